# Optimizing a Trainium2 kernel written in Bass

```python
import jax, jax.numpy as jnp
from jax import lax
import numpy as np


D_MODEL = 4096
BATCH = 4
SEQ = 4096
DEPTH = 1
DEC_BATCH = 8
DEC_SEQ = 2048
PAST_LEN = 128

M_HEADS = 8
M_DQK = 256
M_DV = 512
M_CHUNK = 64
F_GROUPS = 4
F_GROUP_DIM = 1024
F_WIDTH = F_GROUPS * F_GROUP_DIM
D_FF = 11008
CONV_W = 3
EPS = 1e-6

QK_W = M_HEADS * M_DQK
V_W = M_HEADS * M_DV
OFF_Q = 0
OFF_K = OFF_Q + QK_W
OFF_V = OFF_K + QK_W
OFF_O = OFF_V + V_W
OFF_I = OFF_O + V_W
OFF_F = OFF_I + 2 * M_HEADS
OFF_FR = OFF_F + 2 * M_HEADS
OFF_GM = OFF_FR + F_WIDTH
OFF_GF = OFF_GM + D_MODEL
IN_W = OFF_GF + D_MODEL

kernel_name = 'hybrid_mlstm_fnet_encoder'


def _rmsnorm(x, g):
    xf = x.astype(jnp.float32)
    y = xf * lax.rsqrt(jnp.mean(xf * xf, axis=-1, keepdims=True) + EPS)
    return (y * g.astype(jnp.float32)).astype(x.dtype)


def _mlstm_dir(q, k, v, ig, lf):
    B, H, S, _ = q.shape
    nc = S // M_CHUNK

    def chunks(a):
        a = a.reshape((B, H, nc, M_CHUNK) + a.shape[3:])
        return jnp.moveaxis(a, 2, 0)

    lower = jnp.tril(jnp.ones((M_CHUNK, M_CHUNK), dtype=bool))

    def step(carry, inp):
        C, n, m = carry
        qc, kc, vc, ic, fc = inp
        b = jnp.cumsum(fc, axis=-1)
        dmat = b[..., :, None] - b[..., None, :] + ic[..., None, :]
        dmat = jnp.where(lower, dmat, -jnp.inf)
        m_inter = b + m[..., None]
        m_t = jnp.maximum(m_inter, jnp.max(dmat, axis=-1))
        s = jnp.einsum('bhtd,bhsd->bhts', qc, kc) * jnp.exp(dmat - m_t[..., None])
        sc = jnp.exp(m_inter - m_t)
        num = sc[..., None] * jnp.einsum('bhvd,bhtd->bhtv', C, qc) + jnp.einsum('bhts,bhsv->bhtv', s, vc)
        den = sc * jnp.einsum('bhd,bhtd->bht', n, qc) + jnp.sum(s, axis=-1)
        h = num / jnp.maximum(jnp.abs(den), jnp.exp(-m_t))[..., None]
        g = b[..., -1]
        a = g[..., None] - b + ic
        m_new = jnp.maximum(g + m, jnp.max(a, axis=-1))
        decay = jnp.exp(g + m - m_new)
        wa = jnp.exp(a - m_new[..., None])
        C = decay[..., None, None] * C + jnp.einsum('bhsv,bhsd->bhvd', wa[..., None] * vc, kc)
        n = decay[..., None] * n + jnp.einsum('bhs,bhsd->bhd', wa, kc)
        return (C, n, m_new), h

    init = (jnp.zeros((B, H, M_DV, M_DQK), jnp.float32),
            jnp.zeros((B, H, M_DQK), jnp.float32),
            jnp.zeros((B, H), jnp.float32))
    _, hs = lax.scan(step, init, (chunks(q), chunks(k), chunks(v), chunks(ig), chunks(lf)))
    return jnp.moveaxis(hs, 0, 2).reshape(B, H, S, M_DV)


def _flip(a):
    return jnp.flip(a, axis=2)


def _mixer(xn, w_in, b_ig, b_fg, mh_g, w_mo, w_fo, b_merge, w_out):
    B, S, _ = xn.shape
    f32 = jnp.float32
    p = xn @ w_in

    def heads(a, d):
        return a.reshape(B, S, M_HEADS, d).transpose(0, 2, 1, 3).astype(f32)

    q = heads(p[..., OFF_Q:OFF_K], M_DQK)
    k = heads(p[..., OFF_K:OFF_V], M_DQK) * (M_DQK ** -0.5)
    v = heads(p[..., OFF_V:OFF_O], M_DV)
    ig = (p[..., OFF_I:OFF_F].astype(f32).reshape(B, S, 2, M_HEADS) + b_ig.astype(f32)).transpose(2, 0, 3, 1)
    lf = jax.nn.log_sigmoid(p[..., OFF_F:OFF_FR].astype(f32).reshape(B, S, 2, M_HEADS) + b_fg.astype(f32)).transpose(2, 0, 3, 1)
    h = _mlstm_dir(q, k, v, ig[0], lf[0]) + _flip(_mlstm_dir(_flip(q), _flip(k), _flip(v), _flip(ig[1]), _flip(lf[1])))
    h = h * lax.rsqrt(jnp.mean(h * h, axis=-1, keepdims=True) + EPS)
    h = h.transpose(0, 2, 1, 3).reshape(B, S, V_W) * mh_g.astype(f32)
    h = h * jax.nn.sigmoid(p[..., OFF_O:OFF_I].astype(f32))
    h_m = h.astype(xn.dtype) @ w_mo
    fr = p[..., OFF_FR:OFF_GM].astype(f32).reshape(B, S, F_GROUPS, F_GROUP_DIM).transpose(0, 2, 1, 3)
    fr = jnp.fft.fft2(fr, axes=(-2, -1), norm='ortho').real
    fr = fr.transpose(0, 2, 1, 3).reshape(B, S, F_WIDTH).astype(xn.dtype)
    h_f = fr @ w_fo
    gates = jax.nn.sigmoid(p[..., OFF_GM:IN_W].astype(f32) + b_merge.astype(f32))
    merged = gates[..., :D_MODEL] * h_m.astype(f32) + gates[..., D_MODEL:] * h_f.astype(f32)
    return merged.astype(xn.dtype) @ w_out


def _ffn(xn, w_up, cw, cb, w_down):
    u = xn @ w_up
    u = lax.conv_general_dilated(u, cw[:, None, :], window_strides=(1,), padding=((CONV_W // 2, CONV_W // 2),),
                                 dimension_numbers=('NWC', 'WIO', 'NWC'), feature_group_count=2 * D_FF) + cb
    a = u[..., :D_FF].astype(jnp.float32)
    val = u[..., D_FF:].astype(jnp.float32)
    h = jax.nn.gelu(a, approximate=False) * val
    return h.astype(xn.dtype) @ w_down


def _trunk(x, norm_mix, w_in, b_igate, b_fgate, mh_norm, w_mlstm_out, w_fourier_out, b_merge, w_out,
           norm_ffn, w_up, conv_w, conv_b, w_down, norm_final):
    for l in range(DEPTH):
        x = x + _mixer(_rmsnorm(x, norm_mix[l]), w_in[l], b_igate[l], b_fgate[l], mh_norm[l],
                       w_mlstm_out[l], w_fourier_out[l], b_merge[l], w_out[l])
        x = x + _ffn(_rmsnorm(x, norm_ffn[l]), w_up[l], conv_w[l], conv_b[l], w_down[l])
    return _rmsnorm(x, norm_final)


def setup_inputs(seed: int = 0) -> dict:
    key = jax.random.key(seed)
    ks = jax.random.split(key, 20)
    f32 = jnp.float32

    def nrm(k, shape, scale):
        return jax.random.normal(k, shape, f32) * scale

    b_f = jnp.broadcast_to(jnp.linspace(3.0, 6.0, M_HEADS, dtype=f32), (DEPTH, 2, M_HEADS))
    return {
        'x_prompt': nrm(ks[0], (BATCH, SEQ, D_MODEL), 1.0),
        'x_sample': nrm(ks[1], (DEC_BATCH, DEC_SEQ, D_MODEL), 1.0),
        'norm_mix': 1.0 + nrm(ks[2], (DEPTH, D_MODEL), 0.02),
        'w_in': nrm(ks[3], (DEPTH, D_MODEL, IN_W), D_MODEL ** -0.5),
        'b_igate': nrm(ks[4], (DEPTH, 2, M_HEADS), 0.1),
        'b_fgate': b_f + nrm(ks[5], (DEPTH, 2, M_HEADS), 0.1),
        'mh_norm': 1.0 + nrm(ks[6], (DEPTH, V_W), 0.02),
        'w_mlstm_out': nrm(ks[7], (DEPTH, V_W, D_MODEL), V_W ** -0.5),
        'w_fourier_out': nrm(ks[8], (DEPTH, F_WIDTH, D_MODEL), F_WIDTH ** -0.5),
        'b_merge': nrm(ks[9], (DEPTH, 2 * D_MODEL), 0.02),
        'w_out': nrm(ks[10], (DEPTH, D_MODEL, D_MODEL), D_MODEL ** -0.5),
        'norm_ffn': 1.0 + nrm(ks[11], (DEPTH, D_MODEL), 0.02),
        'w_up': nrm(ks[12], (DEPTH, D_MODEL, 2 * D_FF), D_MODEL ** -0.5),
        'conv_w': nrm(ks[13], (DEPTH, CONV_W, 2 * D_FF), CONV_W ** -0.5),
        'conv_b': nrm(ks[14], (DEPTH, 2 * D_FF), 0.02),
        'w_down': nrm(ks[15], (DEPTH, D_FF, D_MODEL), D_FF ** -0.5),
        'norm_final': 1.0 + nrm(ks[16], (D_MODEL,), 0.02),
    }


def reference(x_prompt, x_sample, norm_mix, w_in, b_igate, b_fgate, mh_norm, w_mlstm_out, w_fourier_out,
              b_merge, w_out, norm_ffn, w_up, conv_w, conv_b, w_down, norm_final):
    y_prompt = _trunk(x_prompt, norm_mix, w_in, b_igate, b_fgate, mh_norm, w_mlstm_out, w_fourier_out,
                      b_merge, w_out, norm_ffn, w_up, conv_w, conv_b, w_down, norm_final)
    y_sample = _trunk(x_sample, norm_mix, w_in, b_igate, b_fgate, mh_norm, w_mlstm_out, w_fourier_out,
                      b_merge, w_out, norm_ffn, w_up, conv_w, conv_b, w_down, norm_final)
    return (y_prompt, y_sample)
```

```python
import math
from contextlib import ExitStack
import numpy as np
import ml_dtypes
import concourse.bass as bass
import concourse.mybir as mybir
from concourse.bass_utils import run_bass_kernel_spmd

F32 = mybir.dt.float32
BF16 = mybir.dt.bfloat16
AF = mybir.ActivationFunctionType
ALU = mybir.AluOpType

D = 4096
H = 8
DQK = 256
DV = 512
L = 64
OFF_Q, OFF_K, OFF_V, OFF_O, OFF_I, OFF_F, OFF_FR, OFF_GM, OFF_GF, IN_W = (
    0, 2048, 4096, 8192, 12288, 12304, 12320, 16416, 20512, 24608)
DFF = 11008
FT = DFF // 128
EPS = 1e-6
KT = D // 128
KTC = 8
CW = 512
NWS = 4


class Buf:
    __slots__ = ("name", "r", "w", "dslot")

    def __init__(self, name):
        self.name = name
        self.r = {}
        self.w = {}
        self.dslot = None


class SemSlot:
    __slots__ = ("sem", "count")

    def __init__(self, sem):
        self.sem = sem
        self.count = 0


def _merge(d, tok):
    sem, val = tok
    k = id(sem)
    if k not in d or d[k][1] < val:
        d[k] = (sem, val)


class EngQ:
    def __init__(self, ctx, eng, name):
        self.eng = eng
        self.name = name
        self.sem = ctx.es.enter_context(ctx.nc.semaphore("q_" + name))
        self.count = 0
        self.waited = {}
        self.ninst = 0

    def wait_tokens(self, toks):
        for sem, val in toks:
            if sem is self.sem and self.name == "tensor":
                continue
            k = id(sem)
            if self.waited.get(k, 0) < val:
                self.eng.wait_ge(sem, val)
                self.waited[k] = val
                self.ninst += 1


class Ctx:
    def __init__(self, nc):
        self.nc = nc
        self.es = ExitStack()
        self.q = {n: EngQ(self, getattr(nc, n), n) for n in ("tensor", "vector", "scalar", "gpsimd", "sync")}
        self.free_slots = []
        self.nslots = 0
        self.bufs = []
        self.pes = None
        self.uid = 0

    def begin_phase(self):
        self.pes = ExitStack()
        self.phase_bufs = []

    def end_phase(self):
        self.barrier()
        for b in self.phase_bufs:
            if b.dslot is not None:
                self.free_slots.append(b.dslot)
                b.dslot = None
        self.pes.close()
        self.pes = None

    def sbuf(self, name, shape, dtype):
        self.uid += 1
        return self.pes.enter_context(self.nc.sbuf_tensor(f"{name}_{self.uid}", list(shape), dtype))

    def psum(self, name, shape, dtype=F32):
        self.uid += 1
        return self.pes.enter_context(self.nc.psum_tensor(f"{name}_{self.uid}", list(shape), dtype))

    def buf(self, name, persistent=False):
        b = Buf(name)
        self.bufs.append(b)
        if not persistent and self.pes is not None:
            self.phase_bufs.append(b)
        return b

    def bufs_n(self, name, n):
        return [self.buf(f"{name}{i}") for i in range(n)]

    def _slot(self, b):
        if b.dslot is None:
            if self.free_slots:
                b.dslot = self.free_slots.pop()
            else:
                self.nslots += 1
                b.dslot = SemSlot(self.es.enter_context(self.nc.semaphore(f"d{self.nslots}")))
        return b.dslot

    def _deps(self, reads, writes):
        d = {}
        for b in reads:
            for t in b.w.values():
                _merge(d, t)
        for b in writes:
            for t in b.w.values():
                _merge(d, t)
            for t in b.r.values():
                _merge(d, t)
        return list(d.values())

    def op(self, qname, fn, reads=(), writes=(), signal=True):
        q = self.q[qname]
        q.wait_tokens(self._deps(reads, writes))
        ins = fn(q.eng)
        q.ninst += 1
        if signal:
            q.count += 1
            ins.then_inc(q.sem, 1)
            tok = (q.sem, q.count)
        else:
            tok = (q.sem, q.count + 1)
        for b in reads:
            _merge(b.r, tok)
        for b in writes:
            _merge(b.w, tok)
        return tok

    def dma(self, qname, out, in_, owner, reads=(), writes=()):
        q = self.q[qname]
        slot = self._slot(owner)
        q.wait_tokens(self._deps(reads, writes))
        ins = q.eng.dma_start(out=out, in_=in_)
        q.ninst += 1
        slot.count += 16
        ins.then_inc(slot.sem, 16)
        tok = (slot.sem, slot.count)
        for b in reads:
            _merge(b.r, tok)
        for b in writes:
            _merge(b.w, tok)
        return tok

    def _all_tokens(self):
        d = {}
        for b in self.bufs:
            for t in b.w.values():
                _merge(d, t)
            for t in b.r.values():
                _merge(d, t)
        return list(d.values())

    def barrier(self):
        toks = self._all_tokens()
        for q in self.q.values():
            q.wait_tokens(toks)
        for b in self.bufs:
            b.r.clear()
            b.w.clear()
        self.bufs = [b for b in self.bufs if b not in self.phase_bufs] if self.pes is not None else self.bufs

    def close(self):
        self.es.close()


class Res:
    def __init__(self, c, wring=True, nstage=4):
        self.c = c
        self.ps = [c.psum(f"ps{i}", [128, 512], F32) for i in range(8)]
        self.b_ps = c.bufs_n("ps", 8)
        self.nacc = 0
        if wring:
            self.wt = [c.sbuf(f"wt{i}", [128, KTC, CW], BF16) for i in range(NWS)]
            self.b_wt = c.bufs_n("wt", NWS)
            self.nw = 0
        self.st32 = [c.sbuf(f"st32_{i}", [128, 512], F32) for i in range(nstage)]
        self.b_st32 = c.bufs_n("st32_", nstage)
        self.st16 = [c.sbuf(f"st16_{i}", [128, 512], BF16) for i in range(nstage)]
        self.b_st16 = c.bufs_n("st16_", nstage)
        self.nst = 0
        self.nev = 0

    def stage(self, dtype):
        i = self.nst % len(self.st32)
        self.nst += 1
        if dtype == F32:
            return self.st32[i], self.b_st32[i]
        return self.st16[i], self.b_st16[i]

    def evq(self):
        self.nev += 1
        return "scalar" if self.nev % 2 else "vector"


def copy_op(qn, out, in_):
    if qn == "scalar":
        return lambda e: e.activation(out, in_, AF.Copy)
    return lambda e: e.tensor_copy(out, in_)


def gemm(c, R, act, b_act, kt_n, mb, w_ap, col0, ncols, orient, epi, tok_cols=None):
    nkc = (kt_n + KTC - 1) // KTC
    if tok_cols is None:
        step = 512 if orient == "A" else 128
        tok_cols = [(s, min(step, mb - s)) for s in range(0, mb, step)]
    for c0 in range(col0, col0 + ncols, CW):
        cw = min(CW, col0 + ncols - c0)
        if orient == "A":
            accs = [(fi, ti) for fi in range((cw + 127) // 128) for ti in range(len(tok_cols))]
        else:
            accs = [(0, ti) for ti in range(len(tok_cols))]
        for a0 in range(0, len(accs), 4):
            grp = accs[a0:a0 + 4]
            base = 4 * (R.nacc % 2)
            R.nacc += 1
            for kc in range(nkc):
                k0 = kc * KTC
                kn = min(KTC, kt_n - k0)
                s = R.nw % NWS
                R.nw += 1
                src = w_ap[k0 * 128:(k0 + kn) * 128, c0:c0 + cw].rearrange("(kt p) n -> p kt n", p=128)
                c.dma("gpsimd", R.wt[s][:, 0:kn, 0:cw], src, R.b_wt[s], writes=[R.b_wt[s]])
                for gi, (fi, ti) in enumerate(grp):
                    ts, tn = tok_cols[ti]
                    bank = base + gi
                    for kk in range(kn):
                        kt = k0 + kk
                        last = (kt == kt_n - 1)
                        if orient == "A":
                            fn_ = min(128, cw - fi * 128)
                            c.op("tensor", lambda e: e.matmul(R.ps[bank][0:fn_, 0:tn], R.wt[s][:, kk, fi * 128:fi * 128 + fn_],
                                                              act[:, kt, ts:ts + tn], start=(kt == 0), stop=last),
                                 reads=[b_act, R.b_wt[s]], writes=[R.b_ps[bank]], signal=(last or kk == kn - 1))
                        else:
                            c.op("tensor", lambda e: e.matmul(R.ps[bank][0:tn, 0:cw], act[:, kt, ts:ts + tn],
                                                              R.wt[s][:, kk, 0:cw], start=(kt == 0), stop=last),
                                 reads=[b_act, R.b_wt[s]], writes=[R.b_ps[bank]], signal=(last or kk == kn - 1))
            for gi, (fi, ti) in enumerate(grp):
                ts, tn = tok_cols[ti]
                bank = base + gi
                if orient == "A":
                    fn_ = min(128, cw - fi * 128)
                    epi(R.ps[bank][0:fn_, 0:tn], R.b_ps[bank], ti, ts, tn, c0 + fi * 128, fn_)
                else:
                    epi(R.ps[bank][0:tn, 0:cw], R.b_ps[bank], ti, ts, tn, c0, cw)


def load_block(c, dst, b_dst, src, t0, mb, kt_n):
    for k0 in range(0, kt_n, 8):
        kn = min(8, kt_n - k0)
        c.dma("sync", dst[:, k0:k0 + kn, 0:mb],
              src[k0 * 128:(k0 + kn) * 128, t0:t0 + mb].rearrange("(kt p) t -> p kt t", p=128),
              b_dst, writes=[b_dst])


class TransposeStore:
    def __init__(self, c, R, ident, b_ident, dst):
        self.c, self.R, self.ident, self.b_ident, self.dst = c, R, ident, b_ident, dst
        self.stg = [c.sbuf(f"tstg{i}", [128, KT, 512], BF16) for i in range(2)]
        self.b_stg = c.bufs_n("tstg", 2)
        self.ng = 0

    def tile(self, xb, b_xb, tt, ntiles):
        c, R = self.c, self.R
        grp = tt // 4
        s = grp % 2
        j4 = tt % 4
        for g in range(KT // 8):
            bank = self.ng % 8
            self.ng += 1
            pst = R.ps[bank][:].bitcast(BF16)
            for j in range(8):
                kt = g * 8 + j
                c.op("tensor", lambda e: e.transpose(pst[:, j * 128:(j + 1) * 128], xb[:, kt * 128:(kt + 1) * 128], self.ident[:]),
                     reads=[b_xb, self.b_ident], writes=[R.b_ps[bank]], signal=(j == 7))
            qn = R.evq()
            c.op(qn, copy_op(qn, self.stg[s][:, g * 8:(g + 1) * 8, j4 * 128:(j4 + 1) * 128],
                             pst.rearrange("p (a b) -> p a b", a=8)),
                 reads=[R.b_ps[bank]], writes=[self.b_stg[s]])
        if j4 == 3 or tt == ntiles - 1:
            ncol = (j4 + 1) * 128
            t0 = grp * 512
            for k0 in range(0, KT, 8):
                c.dma("sync", self.dst["ap"][k0 * 128:(k0 + 8) * 128, t0:t0 + ncol].rearrange("(kt p) t -> p kt t", p=128),
                      self.stg[s][:, k0:k0 + 8, 0:ncol], self.b_stg[s], reads=[self.b_stg[s]], writes=[self.dst["buf"]])


def rms_rstd(c, ss, b_ss, n):
    c.op("vector", lambda e: e.tensor_scalar(ss, ss, 1.0 / n, EPS, ALU.mult, ALU.add), reads=[b_ss], writes=[b_ss])
    c.op("scalar", lambda e: e.activation(ss, ss, AF.Sqrt), reads=[b_ss], writes=[b_ss])
    c.op("vector", lambda e: e.reciprocal(ss, ss), reads=[b_ss], writes=[b_ss])


def build_program(TOK, MB=512):
    assert TOK % MB == 0 and MB == 512
    NB = TOK // MB
    NT = TOK // 128
    NCH = TOK // L
    nc = bass.Bass("TRN2", target_bir_lowering=False)

    def din(name, shape, dt=F32):
        return nc.dram_tensor(name, list(shape), dt, kind="ExternalInput").ap()

    x = din("x", [TOK, D])
    w_in = din("w_in", [D, IN_W])
    w_mo = din("w_mo", [D, D])
    w_fo = din("w_fo", [D, D])
    w_out = din("w_out", [D, D])
    w_up = din("w_up", [D, 2 * DFF])
    w_down = din("w_down", [DFF, D])
    g_mix = din("g_mix", [128, D])
    g_ffn = din("g_ffn", [128, D])
    g_fin = din("g_fin", [128, D])
    g_mh = din("g_mh", [128, D])
    bgate_d = din("bgate", [64, 32])
    bm_d = din("bm", [128, 64])
    cw_d = din("cw", [128, 3, 2 * FT])
    cb_d = din("cb", [128, 2 * FT])
    ident_d = din("ident", [128, 128])
    tri_d = din("tri", [64, 2, 64])
    mask_d = din("mask", [64, 2, 64])
    flag_d = din("flag", [128, 2])
    dftc_d = din("dftc", [TOK, TOK], BF16)
    dfts_d = din("dfts", [TOK, TOK], BF16)
    cc_d = din("cc", [1024, 1024], BF16)
    scn_d = din("scn", [1024, 1024], BF16)
    out = nc.dram_tensor("out", [TOK, D], F32, kind="ExternalOutput").ap()

    c = Ctx(nc)

    def scratch(name, shape, dt):
        return {"ap": nc.dram_tensor(name, list(shape), dt).ap(), "buf": c.buf(name, persistent=True)}

    xnT = scratch("s_xnT", [D, TOK], BF16)
    qT = scratch("s_qT", [2048, TOK], BF16)
    kT = scratch("s_kT", [2048, TOK], BF16)
    kk = scratch("s_kk", [TOK, 2048], BF16)
    vv = scratch("s_vv", [TOK, D], BF16)
    og = scratch("s_og", [TOK, D], F32)
    gif = scratch("s_gif", [TOK, 32], F32)
    frX = scratch("s_frX", [TOK, D], BF16)
    gmT = scratch("s_gmT", [D, TOK], F32)
    gfT = scratch("s_gfT", [D, TOK], F32)
    hdir = [scratch("s_hF", [TOK, D], F32), scratch("s_hB", [TOK, D], F32)]
    hgT = scratch("s_hgT", [D, TOK], BF16)
    zT = scratch("s_zT", [D, TOK], BF16)
    mgT = scratch("s_mgT", [D, TOK], BF16)
    x1 = scratch("s_x1", [TOK, D], F32)
    xn2T = scratch("s_xn2T", [D, TOK], BF16)
    outb = {"ap": out, "buf": c.buf("out", persistent=True)}
    b_in = c.buf("inputs", persistent=True)

    def load_const(name, shape, src, dt=F32, q="sync"):
        t = c.sbuf(name, shape, dt)
        b = c.buf(name)
        c.dma(q, t[:], src, b, writes=[b])
        return t, b

    def load_ident():
        idf, b_idf = load_const("identf", [128, 128], ident_d)
        idb = c.sbuf("identb", [128, 128], BF16)
        b_idb = c.buf("identb")
        c.op("vector", lambda e: e.tensor_copy(idb[:], idf[:]), reads=[b_idf], writes=[b_idb])
        return idb, b_idb

    def phase_norm_T(src, gsrc, dst):
        c.begin_phase()
        R = Res(c, wring=False, nstage=1)
        idb, b_idb = load_ident()
        gbc, b_g = load_const("gbc", [128, D], gsrc)
        TS = TransposeStore(c, R, idb, b_idb, dst)
        xt = [c.sbuf(f"xt{i}", [128, D], F32) for i in range(2)]
        b_xt = c.bufs_n("xt", 2)
        xb = [c.sbuf(f"xb{i}", [128, D], BF16) for i in range(2)]
        b_xb = c.bufs_n("xb", 2)
        ss = [c.sbuf(f"ss{i}", [128, 1], F32) for i in range(2)]
        b_ss = c.bufs_n("ss", 2)
        for tt in range(NT):
            s = tt % 2
            c.dma("sync", xt[s][:], src["ap"][tt * 128:(tt + 1) * 128, :], b_xt[s], reads=[src["buf"]], writes=[b_xt[s]])
            c.op("scalar", lambda e: e.activation(xb[s][:], xt[s][:], AF.Square, accum_out=ss[s][:]),
                 reads=[b_xt[s]], writes=[b_xb[s], b_ss[s]])
            rms_rstd(c, ss[s][:], b_ss[s], D)
            c.op("vector", lambda e: e.scalar_tensor_tensor(xb[s][:], xt[s][:], ss[s][:, 0:1], gbc[:], ALU.mult, ALU.mult),
                 reads=[b_xt[s], b_ss[s], b_g], writes=[b_xb[s]])
            TS.tile(xb[s], b_xb[s], tt, NT)
        c.end_phase()

    def phase_A():
        c.begin_phase()
        R = Res(c)
        act = c.sbuf("actA", [128, KT, MB], BF16)
        b_act = c.buf("actA")

        def store_epi(dst, dt, orient):
            def epi(ps, b_ps, ti, ts, tn, c0, ncol, dst=dst, dt=dt):
                stg, b_stg = R.stage(dt)
                qn = R.evq()
                if orient == "A":
                    sv = stg[0:ncol, 0:tn]
                    dv = dst["ap"][c0 - dst["c0"]:c0 - dst["c0"] + ncol, t0 + ts:t0 + ts + tn]
                else:
                    sv = stg[0:tn, 0:ncol]
                    dv = dst["ap"][t0 + ts:t0 + ts + tn, c0 - dst["c0"]:c0 - dst["c0"] + ncol]
                c.op(qn, copy_op(qn, sv, ps), reads=[b_ps], writes=[b_stg])
                c.dma("sync", dv, sv, b_stg, reads=[b_stg], writes=[dst["buf"]])
            return epi

        segs = [
            (OFF_Q, 2048, "A", qT, BF16), (OFF_K, 2048, "A", kT, BF16), (OFF_K, 2048, "B", kk, BF16),
            (OFF_V, 4096, "B", vv, BF16), (OFF_O, 4096, "B", og, F32), (OFF_I, 32, "B", gif, F32),
            (OFF_FR, 4096, "B", frX, BF16), (OFF_GM, 4096, "A", gmT, F32), (OFF_GF, 4096, "A", gfT, F32),
        ]
        for blk in range(NB):
            t0 = blk * MB
            load_block(c, act, b_act, xnT["ap"], t0, MB, KT)
            for (c0, n, orient, dst, dt) in segs:
                dst["c0"] = c0
                gemm(c, R, act, b_act, KT, MB, w_in, c0, n, orient, store_epi(dst, dt, orient))
        c.end_phase()

    def phase_B():
        c.begin_phase()
        ps = [c.psum(f"psB{i}", [128, 512], F32) for i in range(8)]
        b_psS = c.bufs_n("psS", H)
        b_psG = c.buf("psG")
        b_psD = c.buf("psD")
        b_psn = c.buf("psn")
        b_psN = c.bufs_n("psN", 2)
        b_psC = c.bufs_n("psC", 4)
        tri, b_tri = load_const("tri", [64, 2, 64], tri_d)
        mask, b_mask = load_const("mask", [64, 2, 64], mask_d)
        bgate, b_bgate = load_const("bgate", [64, 32], bgate_d)
        flag, b_flag = load_const("flag", [128, 2], flag_d)
        onesn = c.sbuf("onesn", [64, 128], F32)
        ones16 = c.sbuf("ones16", [64, 1], BF16)
        b_ones = c.buf("ones")
        c.op("vector", lambda e: e.memset(onesn[:], -1.0), writes=[b_ones])
        c.op("vector", lambda e: e.memset(ones16[:], 1.0), writes=[b_ones])
        C32 = c.sbuf("C32", [128, H, 2, DV], F32)
        Cb = c.sbuf("Cb", [128, H, 2, DV], BF16)
        n32 = c.sbuf("n32", [128, 2 * H], F32)
        nb = c.sbuf("nb", [128, 2 * H], BF16)
        b_C = [c.buf(f"C{h}") for h in range(H)]
        b_Cb = [c.buf(f"Cb{h}") for h in range(H)]
        b_n = c.buf("n32")
        b_nb = c.buf("nb")
        qs = [c.sbuf(f"qs{i}", [128, 16, 128], BF16) for i in range(2)]
        ks = [c.sbuf(f"ks{i}", [128, 16, 128], BF16) for i in range(2)]
        kks = [c.sbuf(f"kks{i}", [64, 2, 2048], BF16) for i in range(2)]
        vvs = [c.sbuf(f"vvs{i}", [64, 2, D], BF16) for i in range(2)]
        gis = [c.sbuf(f"gis{i}", [64, 2, 32], F32) for i in range(2)]
        b_in_s = c.bufs_n("mls_in", 2)
        hbuf = [c.sbuf(f"hbuf{i}", [64, D], F32) for i in range(2)]
        b_hbuf = c.bufs_n("hbuf", 2)
        vt = [c.sbuf(f"vt{i}", [64, DV], BF16) for i in range(2)]
        b_vt = c.bufs_n("vt", 2)
        PT = [c.sbuf(f"PT{i}", [64, H, 64], BF16) for i in range(2)]
        b_PT = c.bufs_n("PT", 2)
        gt = [c.sbuf(f"gt{i}", [128, 80], F32) for i in range(2)]
        b_gt = c.bufs_n("gt", 2)
        ab = [c.sbuf(f"ab{i}", [64, H], BF16) for i in range(2)]
        nvt = 0
        for d in range(2):
            c.op("vector", lambda e: e.memset(C32[:], 0.0), writes=b_C)
            c.op("gpsimd", lambda e: e.memset(Cb[:], 0.0), writes=b_Cb)
            c.op("vector", lambda e: e.memset(n32[:], 0.0), writes=[b_n])
            c.op("vector", lambda e: e.memset(nb[:], 0.0), writes=[b_nb])
            order = list(range(NCH)) if d == 0 else list(range(NCH - 1, -1, -1))
            reset_at = NCH // 2 if d == 0 else NCH // 2 - 1
            loaded = -1
            for step, ci in enumerate(order):
                sc = ci // 2
                lc = ci % 2
                s = (step // 2) % 2
                if sc != loaded:
                    loaded = sc
                    t0 = sc * 128
                    bi = b_in_s[s]
                    c.dma("sync", qs[s][:], qT["ap"][:, t0:t0 + 128].rearrange("(kt p) t -> p kt t", p=128), bi, reads=[qT["buf"]], writes=[bi])
                    c.dma("sync", ks[s][:], kT["ap"][:, t0:t0 + 128].rearrange("(kt p) t -> p kt t", p=128), bi, reads=[kT["buf"]], writes=[bi])
                    c.dma("sync", kks[s][:], kk["ap"][t0:t0 + 128, :].rearrange("(c p) n -> p c n", p=64), bi, reads=[kk["buf"]], writes=[bi])
                    c.dma("sync", vvs[s][:], vv["ap"][t0:t0 + 128, :].rearrange("(c p) n -> p c n", p=64), bi, reads=[vv["buf"]], writes=[bi])
                    c.dma("sync", gis[s][:], gif["ap"][t0:t0 + 128, :].rearrange("(c p) n -> p c n", p=64), bi, reads=[gif["buf"]], writes=[bi])
                bi = b_in_s[s]
                cs = slice(lc * 64, lc * 64 + 64)
                if ci == reset_at:
                    c.op("vector", lambda e: e.tensor_scalar(C32[:], C32[:], flag[:, 0:1], None, ALU.mult), reads=[b_flag], writes=b_C)
                    c.op("gpsimd", lambda e: e.tensor_scalar(Cb[:], Cb[:], flag[:, 0:1], None, ALU.mult), reads=[b_flag], writes=b_Cb)
                    c.op("vector", lambda e: e.tensor_scalar(n32[:], n32[:], flag[:, 0:1], None, ALU.mult), reads=[b_flag], writes=[b_n])
                    c.op("vector", lambda e: e.tensor_scalar(nb[:], nb[:], flag[:, 0:1], None, ALU.mult), reads=[b_flag], writes=[b_nb])
                g_ = gt[step % 2]
                bg = b_gt[step % 2]
                a_b = ab[step % 2]
                zt, sp, igt, wt_, et, eg, at, dn, rr = (g_[0:64, 0:8], g_[0:64, 8:16], g_[0:64, 16:24], g_[0:64, 24:32],
                                                        g_[0:64, 32:40], g_[:, 40:48], g_[0:64, 48:56], g_[0:64, 56:64], g_[0:64, 64:72])
                c.op("vector", lambda e: e.tensor_tensor(zt, gis[s][:, lc, 16 + 8 * d:24 + 8 * d], bgate[:, 16 + 8 * d:24 + 8 * d], ALU.add),
                     reads=[bi, b_bgate], writes=[bg])
                c.op("scalar", lambda e: e.activation(sp, zt, AF.Exp, scale=-1.0), reads=[bg], writes=[bg])
                c.op("scalar", lambda e: e.activation(sp, sp, AF.Ln, bias=1.0), reads=[bg], writes=[bg])
                c.op("tensor", lambda e: e.matmul(ps[1][0:64, 0:8], tri[:, d, :], sp, start=True, stop=True),
                     reads=[b_tri, bg], writes=[b_psG])
                c.op("tensor", lambda e: e.matmul(ps[1][:, 8:16], onesn[:], sp, start=True, stop=True),
                     reads=[b_ones, bg], writes=[b_psG])
                c.op("vector", lambda e: e.tensor_tensor(igt, gis[s][:, lc, 8 * d:8 * d + 8], bgate[:, 8 * d:8 * d + 8], ALU.add),
                     reads=[bi, b_bgate], writes=[bg])
                c.op("vector", lambda e: e.tensor_tensor(igt, igt, ps[1][0:64, 0:8], ALU.subtract), reads=[b_psG, bg], writes=[bg])
                c.op("scalar", lambda e: e.activation(wt_, igt, AF.Exp, bias=-math.log(16.0)), reads=[bg], writes=[bg])
                c.op("scalar", lambda e: e.activation(et, ps[1][0:64, 0:8], AF.Exp), reads=[b_psG], writes=[bg])
                c.op("scalar", lambda e: e.activation(eg, ps[1][:, 8:16], AF.Exp), reads=[b_psG], writes=[bg])
                c.op("vector", lambda e: e.tensor_tensor(at, wt_, g_[0:64, 40:48], ALU.mult), reads=[bg], writes=[bg])
                c.op("vector", lambda e: e.tensor_copy(a_b[:], at), reads=[bg], writes=[bg])
                pt = PT[step % 2]
                bpt = b_PT[step % 2]
                for h in range(H):
                    for dt in range(2):
                        c.op("tensor", lambda e: e.matmul(ps[0][0:64, h * 64:(h + 1) * 64], ks[s][:, 2 * h + dt, cs], qs[s][:, 2 * h + dt, cs],
                                                          start=(dt == 0), stop=(dt == 1)),
                             reads=[bi], writes=[b_psS[h]], signal=(dt == 1))
                    c.op("vector", lambda e: e.scalar_tensor_tensor(pt[:, h, :], ps[0][0:64, h * 64:(h + 1) * 64], wt_[:, h:h + 1], mask[:, d, :],
                                                                    ALU.mult, ALU.mult),
                         reads=[b_psS[h], bg, b_mask], writes=[bpt])
                    c.op("tensor", lambda e: e.matmul(ps[1][0:64, 16 + h:17 + h], pt[:, h, :], ones16[:], start=True, stop=False),
                         reads=[bpt, b_ones], writes=[b_psD], signal=False)
                    for dt in range(2):
                        c.op("tensor", lambda e: e.matmul(ps[1][0:64, 16 + h:17 + h], qs[s][:, 2 * h + dt, cs], nb[:, 2 * h + dt:2 * h + dt + 1],
                                                          start=False, stop=(dt == 1)),
                             reads=[bi, b_nb], writes=[b_psD], signal=(dt == 1))
                c.op("vector", lambda e: e.tensor_tensor(dn, ps[1][0:64, 16:24], et, ALU.mult), reads=[b_psD, bg], writes=[bg])
                dn2 = g_[0:64, 72:80]
                c.op("vector", lambda e: e.tensor_scalar(dn2, dn, -1.0, None, ALU.mult), reads=[bg], writes=[bg])
                c.op("vector", lambda e: e.tensor_tensor(dn, dn, dn2, ALU.max), reads=[bg], writes=[bg])
                c.op("vector", lambda e: e.tensor_scalar(dn, dn, 1.0, None, ALU.max), reads=[bg], writes=[bg])
                c.op("vector", lambda e: e.reciprocal(dn, dn), reads=[bg], writes=[bg])
                c.op("vector", lambda e: e.tensor_tensor(rr, dn, et, ALU.mult), reads=[bg], writes=[bg])
                hb = hbuf[step % 2]
                bhb = b_hbuf[step % 2]
                for h in range(H):
                    pn = 2 + (h % 2)
                    c.op("tensor", lambda e: e.matmul(ps[pn][0:64, :], pt[:, h, :], vvs[s][:, lc, h * DV:(h + 1) * DV], start=True, stop=False),
                         reads=[bpt, bi], writes=[b_psN[h % 2]], signal=False)
                    for dt in range(2):
                        c.op("tensor", lambda e: e.matmul(ps[pn][0:64, :], qs[s][:, 2 * h + dt, cs], Cb[:, h, dt, :], start=False, stop=(dt == 1)),
                             reads=[bi, b_Cb[h]], writes=[b_psN[h % 2]], signal=(dt == 1))
                    c.op("scalar", lambda e: e.activation(hb[:, h * DV:(h + 1) * DV], ps[pn][0:64, :], AF.Copy, scale=rr[:, h:h + 1]),
                         reads=[b_psN[h % 2], bg], writes=[bhb])
                    v_ = vt[nvt % 2]
                    bv = b_vt[nvt % 2]
                    nvt += 1
                    c.op("gpsimd", lambda e: e.tensor_scalar(v_[:], vvs[s][:, lc, h * DV:(h + 1) * DV], at[:, h:h + 1], None, ALU.mult),
                         reads=[bi, bg], writes=[bv])
                    for dt in range(2):
                        pc = 4 + 2 * (h % 2) + dt
                        kslice = kks[s][:, lc, h * DQK + dt * 128:h * DQK + dt * 128 + 128]
                        c.op("tensor", lambda e: e.matmul(ps[pc][:, :], kslice, v_[:], start=True, stop=True),
                             reads=[bi, bv], writes=[b_psC[pc - 4]])
                        c.op("tensor", lambda e: e.matmul(ps[1][:, 24 + 2 * h + dt:25 + 2 * h + dt], kslice, a_b[:, h:h + 1], start=True, stop=True),
                             reads=[bi, bg], writes=[b_psn])
                        c.op("vector", lambda e: e.scalar_tensor_tensor(C32[:, h, dt, :], C32[:, h, dt, :], eg[:, h:h + 1], ps[pc][:, :],
                                                                        ALU.mult, ALU.add),
                             reads=[bg, b_psC[pc - 4]], writes=[b_C[h]])
                        cq = "scalar" if dt == 0 else "gpsimd"
                        c.op(cq, copy_op(cq, Cb[:, h, dt, :], C32[:, h, dt, :]), reads=[b_C[h]], writes=[b_Cb[h]])
                n3 = n32[:].rearrange("p (h t) -> p h t", t=2)
                p3 = ps[1][:, 24:40].rearrange("p (h t) -> p h t", t=2)
                for dt in range(2):
                    c.op("vector", lambda e: e.tensor_tensor(n3[:, :, dt], n3[:, :, dt], eg, ALU.mult), reads=[bg], writes=[b_n])
                    c.op("vector", lambda e: e.tensor_tensor(n3[:, :, dt], n3[:, :, dt], p3[:, :, dt], ALU.add), reads=[b_psn], writes=[b_n])
                c.op("vector", lambda e: e.tensor_copy(nb[:], n32[:]), reads=[b_n], writes=[b_nb])
                c.dma("sync", hdir[d]["ap"][ci * 64:(ci + 1) * 64, :], hb[:], bhb, reads=[bhb], writes=[hdir[d]["buf"]])
        c.end_phase()

    def phase_C0():
        c.begin_phase()
        R = Res(c, wring=False, nstage=1)
        idb, b_idb = load_ident()
        gbc, b_g = load_const("gmh", [128, D], g_mh)
        TS = TransposeStore(c, R, idb, b_idb, hgT)
        ha = [c.sbuf(f"ha{i}", [128, D], F32) for i in range(2)]
        hb_ = [c.sbuf(f"hb{i}", [128, D], F32) for i in range(2)]
        ho = [c.sbuf(f"ho{i}", [128, D], F32) for i in range(2)]
        b_ha = c.bufs_n("ha", 2)
        b_hb = c.bufs_n("hb", 2)
        b_ho = c.bufs_n("ho", 2)
        xb = [c.sbuf(f"hx{i}", [128, D], BF16) for i in range(2)]
        b_xb = c.bufs_n("hx", 2)
        ss = [c.sbuf(f"hss{i}", [128, H], F32) for i in range(2)]
        b_ss = c.bufs_n("hss", 2)
        for tt in range(NT):
            s = tt % 2
            rows = slice(tt * 128, (tt + 1) * 128)
            c.dma("sync", ha[s][:], hdir[0]["ap"][rows, :], b_ha[s], reads=[hdir[0]["buf"]], writes=[b_ha[s]])
            c.dma("sync", hb_[s][:], hdir[1]["ap"][rows, :], b_hb[s], reads=[hdir[1]["buf"]], writes=[b_hb[s]])
            c.dma("sync", ho[s][:], og["ap"][rows, :], b_ho[s], reads=[og["buf"]], writes=[b_ho[s]])
            c.op("vector", lambda e: e.tensor_tensor(ha[s][:], ha[s][:], hb_[s][:], ALU.add), reads=[b_hb[s]], writes=[b_ha[s]])
            for h in range(H):
                c.op("scalar", lambda e: e.activation(hb_[s][:, h * DV:(h + 1) * DV], ha[s][:, h * DV:(h + 1) * DV], AF.Square,
                                                      accum_out=ss[s][:, h:h + 1]),
                     reads=[b_ha[s]], writes=[b_hb[s], b_ss[s]])
            rms_rstd(c, ss[s][:], b_ss[s], DV)
            c.op("scalar", lambda e: e.activation(ho[s][:], ho[s][:], AF.Sigmoid), reads=[], writes=[b_ho[s]])
            for h in range(H):
                c.op("vector", lambda e: e.tensor_scalar(ha[s][:, h * DV:(h + 1) * DV], ha[s][:, h * DV:(h + 1) * DV], ss[s][:, h:h + 1], None, ALU.mult),
                     reads=[b_ss[s]], writes=[b_ha[s]])
            c.op("gpsimd", lambda e: e.tensor_tensor(ho[s][:], ho[s][:], gbc[:], ALU.mult), reads=[b_g], writes=[b_ho[s]])
            c.op("vector", lambda e: e.tensor_tensor(xb[s][:], ha[s][:], ho[s][:], ALU.mult), reads=[b_ha[s], b_ho[s]], writes=[b_xb[s]])
            TS.tile(xb[s], b_xb[s], tt, NT)
        c.end_phase()

    def phase_D():
        c.begin_phase()
        R = Res(c, wring=False)
        ST = TOK // 128
        TB = 256
        Xg = c.sbuf("Xg", [128, ST, 1024], BF16)
        b_Xg = c.buf("Xg")
        dcs = [c.sbuf(f"dc{i}", [128, ST, TB], BF16) for i in range(2)]
        dss = [c.sbuf(f"ds{i}", [128, ST, TB], BF16) for i in range(2)]
        b_dft = c.bufs_n("dft", 2)
        ccs, b_cc = load_const("ccs", [128, 8, 1024], cc_d.rearrange("(kt p) n -> p kt n", p=128), BF16)
        scs, b_sc = load_const("scs", [128, 8, 1024], scn_d.rearrange("(kt p) n -> p kt n", p=128), BF16)
        A1 = [c.sbuf(f"A1_{i}", [128, 8, TB], BF16) for i in range(2)]
        A2 = [c.sbuf(f"A2_{i}", [128, 8, TB], BF16) for i in range(2)]
        b_A = c.bufs_n("Aseq", 2)
        nit = 0
        nbk = 0
        for g in range(4):
            for s0 in range(0, ST, 8):
                c.dma("sync", Xg[:, s0:s0 + 8, :], frX["ap"][s0 * 128:(s0 + 8) * 128, g * 1024:(g + 1) * 1024].rearrange("(st p) n -> p st n", p=128),
                      b_Xg, reads=[frX["buf"]], writes=[b_Xg])
            for tb in range(TOK // TB):
                s = nit % 2
                nit += 1
                for s0 in range(0, ST, 8):
                    c.dma("sync", dcs[s][:, s0:s0 + 8, :], dftc_d[s0 * 128:(s0 + 8) * 128, tb * TB:(tb + 1) * TB].rearrange("(st p) n -> p st n", p=128),
                          b_dft[s], writes=[b_dft[s]])
                    c.dma("sync", dss[s][:, s0:s0 + 8, :], dfts_d[s0 * 128:(s0 + 8) * 128, tb * TB:(tb + 1) * TB].rearrange("(st p) n -> p st n", p=128),
                          b_dft[s], writes=[b_dft[s]])
                for ct in range(8):
                    b1 = nbk % 8
                    b2 = (nbk + 1) % 8
                    nbk += 2
                    for st in range(ST):
                        c.op("tensor", lambda e: e.matmul(R.ps[b1][:, 0:TB], Xg[:, st, ct * 128:(ct + 1) * 128], dcs[s][:, st, :],
                                                          start=(st == 0), stop=(st == ST - 1)),
                             reads=[b_Xg, b_dft[s]], writes=[R.b_ps[b1]], signal=(st == ST - 1))
                        c.op("tensor", lambda e: e.matmul(R.ps[b2][:, 0:TB], Xg[:, st, ct * 128:(ct + 1) * 128], dss[s][:, st, :],
                                                          start=(st == 0), stop=(st == ST - 1)),
                             reads=[b_Xg, b_dft[s]], writes=[R.b_ps[b2]], signal=(st == ST - 1))
                    c.op("scalar", copy_op("scalar", A1[s][:, ct, :], R.ps[b1][:, 0:TB]), reads=[R.b_ps[b1]], writes=[b_A[s]])
                    c.op("vector", copy_op("vector", A2[s][:, ct, :], R.ps[b2][:, 0:TB]), reads=[R.b_ps[b2]], writes=[b_A[s]])
                for c2 in range(8):
                    bk = nbk % 8
                    nbk += 1
                    for ct in range(8):
                        c.op("tensor", lambda e: e.matmul(R.ps[bk][:, 0:TB], ccs[:, ct, c2 * 128:(c2 + 1) * 128], A1[s][:, ct, :],
                                                          start=(ct == 0), stop=False),
                             reads=[b_cc, b_A[s]], writes=[R.b_ps[bk]], signal=False)
                        c.op("tensor", lambda e: e.matmul(R.ps[bk][:, 0:TB], scs[:, ct, c2 * 128:(c2 + 1) * 128], A2[s][:, ct, :],
                                                          start=False, stop=(ct == 7)),
                             reads=[b_sc, b_A[s]], writes=[R.b_ps[bk]], signal=(ct == 7))
                    stg, b_stg = R.stage(BF16)
                    qn = R.evq()
                    c.op(qn, copy_op(qn, stg[:, 0:TB], R.ps[bk][:, 0:TB]), reads=[R.b_ps[bk]], writes=[b_stg])
                    r0 = g * 1024 + c2 * 128
                    c.dma("sync", zT["ap"][r0:r0 + 128, tb * TB:(tb + 1) * TB], stg[:, 0:TB], b_stg, reads=[b_stg], writes=[zT["buf"]])
        c.end_phase()

    def phase_C():
        c.begin_phase()
        R = Res(c)
        actm = c.sbuf("actm", [128, KT, MB], BF16)
        actf = c.sbuf("actf", [128, KT, MB], BF16)
        b_am = c.buf("actm")
        b_af = c.buf("actf")
        bm, b_bm = load_const("bm", [128, 64], bm_d)
        t1 = [c.sbuf(f"t1_{i}", [128, 512], F32) for i in range(8)]
        b_t1 = c.bufs_n("t1_", 8)
        gl = [c.sbuf(f"gl{i}", [128, 512], F32) for i in range(4)]
        b_gl = c.bufs_n("gl", 4)
        cnt = {"g": 0, "t": 0}
        for blk in range(NB):
            t0 = blk * MB
            load_block(c, actm, b_am, hgT["ap"], t0, MB, KT)
            load_block(c, actf, b_af, zT["ap"], t0, MB, KT)

            def gate(src, boff, c0, ncol, ts, tn):
                i = cnt["g"] % 4
                cnt["g"] += 1
                c.dma("sync", gl[i][0:ncol, 0:tn], src["ap"][c0:c0 + ncol, t0 + ts:t0 + ts + tn], b_gl[i], reads=[src["buf"]], writes=[b_gl[i]])
                ft = boff + c0 // 128
                c.op("scalar", lambda e: e.activation(gl[i][0:ncol, 0:tn], gl[i][0:ncol, 0:tn], AF.Sigmoid, bias=bm[0:ncol, ft:ft + 1]),
                     reads=[b_bm], writes=[b_gl[i]])
                return gl[i], b_gl[i]

            for c0 in range(0, D, CW):
                slots = {}

                def epi1(ps, b_ps, ti, ts, tn, cc0, ncol):
                    g_, bg_ = gate(gmT, 0, cc0, ncol, ts, tn)
                    i = cnt["t"] % 8
                    cnt["t"] += 1
                    slots[(cc0, ti)] = i
                    c.op("vector", lambda e: e.tensor_tensor(t1[i][0:ncol, 0:tn], ps, g_[0:ncol, 0:tn], ALU.mult),
                         reads=[b_ps, bg_], writes=[b_t1[i]])

                def epi2(ps, b_ps, ti, ts, tn, cc0, ncol):
                    g_, bg_ = gate(gfT, 32, cc0, ncol, ts, tn)
                    i = slots[(cc0, ti)]
                    c.op("vector", lambda e: e.tensor_tensor(g_[0:ncol, 0:tn], ps, g_[0:ncol, 0:tn], ALU.mult),
                         reads=[b_ps], writes=[bg_])
                    stg, b_stg = R.stage(BF16)
                    c.op("vector", lambda e: e.tensor_tensor(stg[0:ncol, 0:tn], g_[0:ncol, 0:tn], t1[i][0:ncol, 0:tn], ALU.add),
                         reads=[bg_, b_t1[i]], writes=[b_stg])
                    c.dma("sync", mgT["ap"][cc0:cc0 + ncol, t0 + ts:t0 + ts + tn], stg[0:ncol, 0:tn], b_stg, reads=[b_stg], writes=[mgT["buf"]])

                gemm(c, R, actm, b_am, KT, MB, w_mo, c0, CW, "A", epi1)
                gemm(c, R, actf, b_af, KT, MB, w_fo, c0, CW, "A", epi2)
        c.end_phase()

    def phase_E():
        c.begin_phase()
        R = Res(c)
        act = c.sbuf("actE", [128, KT, MB], BF16)
        b_act = c.buf("actE")
        xl = [c.sbuf(f"xl{i}", [128, 512], F32) for i in range(4)]
        b_xl = c.bufs_n("xl", 4)
        cnt = {"x": 0}
        for blk in range(NB):
            t0 = blk * MB
            load_block(c, act, b_act, mgT["ap"], t0, MB, KT)

            def epi(ps, b_ps, ti, ts, tn, c0, ncol):
                i = cnt["x"] % 4
                cnt["x"] += 1
                rows = slice(t0 + ts, t0 + ts + tn)
                c.dma("sync", xl[i][0:tn, 0:ncol], x[rows, c0:c0 + ncol], b_xl[i], writes=[b_xl[i]])
                c.op("vector", lambda e: e.tensor_tensor(xl[i][0:tn, 0:ncol], ps, xl[i][0:tn, 0:ncol], ALU.add), reads=[b_ps], writes=[b_xl[i]])
                c.dma("sync", x1["ap"][rows, c0:c0 + ncol], xl[i][0:tn, 0:ncol], b_xl[i], reads=[b_xl[i]], writes=[x1["buf"]])

            gemm(c, R, act, b_act, KT, MB, w_out, 0, D, "B", epi)
        c.end_phase()

    def phase_F():
        c.begin_phase()
        R = Res(c, nstage=1)
        act = c.sbuf("actF", [128, KT, MB], BF16)
        b_act = c.buf("actF")
        hT = c.sbuf("hT", [128, FT, MB + 1], BF16)
        b_hT = c.buf("hT")
        cwt, b_cw = load_const("cwt", [128, 3, 2 * FT], cw_d)
        cbt, b_cb = load_const("cbt", [128, 2 * FT], cb_d)
        flag, b_flag = load_const("flagF", [128, 2], flag_d)
        cwm = c.sbuf("cwm", [128, 3, 2 * FT], F32)
        c.op("vector", lambda e: e.tensor_scalar(cwm[:], cwt[:], flag[:, 1:2], None, ALU.mult), reads=[b_cw, b_flag], writes=[b_cw])
        carry = c.sbuf("carry", [128, 2 * FT, 2], F32)
        b_carry = c.buf("carry")
        c.op("vector", lambda e: e.memset(carry[:], 0.0), writes=[b_carry])
        EW = MB + 3
        Eb = [c.sbuf(f"Eb{i}", [128, EW], F32) for i in range(2)]
        b_E = c.bufs_n("Eb", 2)
        accA = [c.sbuf(f"accA{i}", [128, MB + 1], F32) for i in range(4)]
        b_accA = c.bufs_n("accA", 4)
        accV = [c.sbuf(f"accV{i}", [128, MB + 1], F32) for i in range(2)]
        b_accV = c.bufs_n("accV", 2)
        xl = [c.sbuf(f"xlF{i}", [128, 512], F32) for i in range(4)]
        b_xl = c.bufs_n("xlF", 4)
        cnt = {"e": 0, "x": 0}
        jb = (TOK // 2) // MB
        for blk in range(NB):
            t0 = blk * MB
            last_blk = (blk == NB - 1)
            nout = MB + 1 if last_blk else MB
            load_block(c, act, b_act, xn2T["ap"], t0, MB, KT)
            for f0 in range(0, FT, 4):
                nf = min(4, FT - f0)
                pend = {}

                def conv_epi(which):
                    def epi(ps, b_ps, ti, ts, tn, c0, ncol):
                        fcol = (c0 - which * DFF) // 128 + which * FT
                        i = cnt["e"] % 2
                        cnt["e"] += 1
                        E, bE = Eb[i], b_E[i]
                        if which == 0:
                            ia = (fcol - f0) % 4
                            A, bA = accA[ia], b_accA[ia]
                        else:
                            A, bA = accV[i], b_accV[i]
                        c.op("vector", lambda e: e.tensor_copy(E[:, 0:2], carry[:, fcol, :]), reads=[b_carry], writes=[bE])
                        c.op("scalar", lambda e: e.activation(E[:, 2:MB + 2], ps, AF.Copy), reads=[b_ps], writes=[bE])
                        if last_blk:
                            c.op("vector", lambda e: e.memset(E[:, MB + 2:MB + 3], 0.0), writes=[bE])
                        c.op("vector", lambda e: e.tensor_copy(carry[:, fcol, :], E[:, MB:MB + 2]), reads=[bE], writes=[b_carry])
                        c.op("vector", lambda e: e.tensor_scalar(A[:, 0:nout], E[:, 1:nout + 1], cwt[:, 1, fcol:fcol + 1], cbt[:, fcol:fcol + 1],
                                                                 ALU.mult, ALU.add), reads=[bE, b_cw, b_cb], writes=[bA])
                        c.op("vector", lambda e: e.scalar_tensor_tensor(A[:, 0:nout], E[:, 0:nout], cwt[:, 0, fcol:fcol + 1], A[:, 0:nout],
                                                                        ALU.mult, ALU.add), reads=[bE, b_cw], writes=[bA])
                        c.op("vector", lambda e: e.scalar_tensor_tensor(A[:, 0:nout], E[:, 2:nout + 2], cwt[:, 2, fcol:fcol + 1], A[:, 0:nout],
                                                                        ALU.mult, ALU.add), reads=[bE, b_cw], writes=[bA])
                        if blk == jb:
                            c.op("vector", lambda e: e.scalar_tensor_tensor(A[:, 0:1], E[:, 2:3], cwm[:, 2, fcol:fcol + 1], A[:, 0:1],
                                                                            ALU.mult, ALU.add), reads=[bE, b_cw], writes=[bA])
                            c.op("vector", lambda e: e.scalar_tensor_tensor(A[:, 1:2], E[:, 1:2], cwm[:, 0, fcol:fcol + 1], A[:, 1:2],
                                                                            ALU.mult, ALU.add), reads=[bE, b_cw], writes=[bA])
                        if which == 0:
                            c.op("scalar", lambda e: e.activation(A[:, 0:nout], A[:, 0:nout], AF.Gelu), reads=[], writes=[bA])
                            pend[fcol] = (A, bA)
                        else:
                            Aa, bAa = pend[fcol - FT]
                            c.op("vector", lambda e: e.tensor_tensor(hT[:, fcol - FT, 0:nout], Aa[:, 0:nout], A[:, 0:nout], ALU.mult),
                                 reads=[bAa, bA], writes=[b_hT])
                    return epi

                gemm(c, R, act, b_act, KT, MB, w_up, f0 * 128, nf * 128, "A", conv_epi(0))
                gemm(c, R, act, b_act, KT, MB, w_up, DFF + f0 * 128, nf * 128, "A", conv_epi(1))
            cstart = 1 if blk == 0 else 0
            tok_cols = []
            j = cstart
            while j < nout:
                n_ = min(128, nout - j)
                tok_cols.append((j, n_))
                j += n_

            def epiD(ps, b_ps, ti, ts, tn, c0, ncol):
                i = cnt["x"] % 4
                cnt["x"] += 1
                rows = slice(t0 - 1 + ts, t0 - 1 + ts + tn)
                c.dma("sync", xl[i][0:tn, 0:ncol], x1["ap"][rows, c0:c0 + ncol], b_xl[i], reads=[x1["buf"]], writes=[b_xl[i]])
                c.op("vector", lambda e: e.tensor_tensor(xl[i][0:tn, 0:ncol], ps, xl[i][0:tn, 0:ncol], ALU.add), reads=[b_ps], writes=[b_xl[i]])
                c.dma("sync", outb["ap"][rows, c0:c0 + ncol], xl[i][0:tn, 0:ncol], b_xl[i], reads=[b_xl[i]], writes=[outb["buf"]])

            gemm(c, R, hT, b_hT, FT, nout, w_down, 0, D, "B", epiD, tok_cols=tok_cols)
        c.end_phase()

    def phase_G():
        c.begin_phase()
        gbc, b_g = load_const("gfin", [128, D], g_fin)
        xt = [c.sbuf(f"gx{i}", [128, D], F32) for i in range(2)]
        b_xt = c.bufs_n("gx", 2)
        jk = [c.sbuf(f"gj{i}", [128, D], BF16) for i in range(2)]
        b_jk = c.bufs_n("gj", 2)
        ss = [c.sbuf(f"gss{i}", [128, 1], F32) for i in range(2)]
        b_ss = c.bufs_n("gss", 2)
        for tt in range(NT):
            s = tt % 2
            rows = slice(tt * 128, (tt + 1) * 128)
            c.dma("sync", xt[s][:], outb["ap"][rows, :], b_xt[s], reads=[outb["buf"]], writes=[b_xt[s]])
            c.op("scalar", lambda e: e.activation(jk[s][:], xt[s][:], AF.Square, accum_out=ss[s][:]), reads=[b_xt[s]], writes=[b_jk[s], b_ss[s]])
            rms_rstd(c, ss[s][:], b_ss[s], D)
            c.op("vector", lambda e: e.scalar_tensor_tensor(xt[s][:], xt[s][:], ss[s][:, 0:1], gbc[:], ALU.mult, ALU.mult),
                 reads=[b_ss[s], b_g], writes=[b_xt[s]])
            c.dma("sync", outb["ap"][rows, :], xt[s][:], b_xt[s], reads=[b_xt[s]], writes=[outb["buf"]])
        c.end_phase()

    phase_norm_T({"ap": x, "buf": b_in}, g_mix, xnT)
    phase_A()
    phase_B()
    phase_C0()
    phase_D()
    phase_C()
    phase_E()
    phase_norm_T(x1, g_ffn, xn2T)
    phase_F()
    phase_G()
    c.barrier()
    stats = {k: v.ninst for k, v in c.q.items()}
    c.close()
    return nc, stats


def _dft_consts(TOK, two_seq):
    def cs(n):
        j = np.arange(n, dtype=np.int64)
        ang = 2.0 * np.pi * ((j[:, None] * j[None, :]) % n).astype(np.float64) / n
        return np.cos(ang) / np.sqrt(n), np.sin(ang) / np.sqrt(n)
    if two_seq:
        c1, s1 = cs(TOK // 2)
        cS = np.zeros((TOK, TOK)); sS = np.zeros((TOK, TOK))
        h = TOK // 2
        cS[:h, :h] = c1; cS[h:, h:] = c1; sS[:h, :h] = s1; sS[h:, h:] = s1
    else:
        cS, sS = cs(TOK)
    cC, sC = cs(1024)
    bf = ml_dtypes.bfloat16
    return cS.astype(np.float32).astype(bf), sS.astype(np.float32).astype(bf), cC.astype(np.float32).astype(bf), (-sC).astype(np.float32).astype(bf)


def make_core_inputs(xs, two_seq, W, TOK):
    f = np.float32
    r = np.arange(64)
    triF = -(r[:, None] <= r[None, :]).astype(f)
    triB = -(r[:, None] >= r[None, :]).astype(f)
    tri = np.stack([triF, triB], axis=1)
    flag = 0.0 if two_seq else 1.0
    dc, ds, cC, sCn = _dft_consts(TOK, two_seq)
    bc = lambda v: np.ascontiguousarray(np.broadcast_to(np.asarray(v, f).reshape(1, -1), (128, v.size)))
    bgate = np.concatenate([W["b_igate"].reshape(-1), W["b_fgate"].reshape(-1)]).astype(f)
    return {
        "x": np.ascontiguousarray(xs, dtype=f),
        "w_in": W["w_in"][0], "w_mo": W["w_mlstm_out"][0], "w_fo": W["w_fourier_out"][0], "w_out": W["w_out"][0],
        "w_up": W["w_up"][0], "w_down": W["w_down"][0],
        "g_mix": bc(W["norm_mix"][0]), "g_ffn": bc(W["norm_ffn"][0]), "g_fin": bc(W["norm_final"]), "g_mh": bc(W["mh_norm"][0]),
        "bgate": np.ascontiguousarray(np.broadcast_to(bgate.reshape(1, 32), (64, 32))),
        "bm": np.ascontiguousarray(W["b_merge"][0].reshape(64, 128).T.astype(f)),
        "cw": np.ascontiguousarray(W["conv_w"][0].reshape(3, 2 * FT, 128).transpose(2, 0, 1).astype(f)),
        "cb": np.ascontiguousarray(W["conv_b"][0].reshape(2 * FT, 128).T.astype(f)),
        "ident": np.eye(128, dtype=f),
        "tri": np.ascontiguousarray(tri), "mask": np.ascontiguousarray(-tri),
        "flag": np.ascontiguousarray(np.broadcast_to(np.array([[flag, flag - 1.0]], f), (128, 2))),
        "dftc": dc, "dfts": ds, "cc": cC, "scn": sCn,
    }


def kernel(**inputs):
    W = {k: np.asarray(v) for k, v in inputs.items() if k not in ("x_prompt", "x_sample")}
    xp = np.asarray(inputs["x_prompt"], dtype=np.float32)
    xs = np.asarray(inputs["x_sample"], dtype=np.float32)
    TOK = xp.shape[1]
    assert xs.shape[1] * 2 == TOK and xp.shape[0] == 4 and xs.shape[0] == 8
    nc, _ = build_program(TOK)
    in_maps = []
    for i in range(4):
        in_maps.append(make_core_inputs(xp[i], False, W, TOK))
    for i in range(4):
        in_maps.append(make_core_inputs(xs[2 * i:2 * i + 2].reshape(TOK, D), True, W, TOK))
    res = run_bass_kernel_spmd(nc, in_maps, core_ids=list(range(8)))
    outs = [np.asarray(r["out"], dtype=np.float32) for r in res.results]
    y_prompt = np.stack(outs[0:4], axis=0)
    y_sample = np.stack([o.reshape(2, TOK // 2, D) for o in outs[4:8]], axis=0).reshape(8, TOK // 2, D)
    return (y_prompt, y_sample)
```

```python
import math
from contextlib import ExitStack
import numpy as np
import ml_dtypes
import concourse.bass as bass
import concourse.mybir as mybir
from concourse.bass_utils import run_bass_kernel_spmd

F32 = mybir.dt.float32
BF16 = mybir.dt.bfloat16
AF = mybir.ActivationFunctionType
ALU = mybir.AluOpType

D = 4096
H = 8
DQK = 256
DV = 512
L = 64
OFF_Q, OFF_K, OFF_V, OFF_O, OFF_I, OFF_F, OFF_FR, OFF_GM, OFF_GF, IN_W = (
    0, 2048, 4096, 8192, 12288, 12304, 12320, 16416, 20512, 24608)
DFF = 11008
FT = DFF // 128
EPS = 1e-6
KT = D // 128
KTC = 8
CW = 512
NWS = 4


class Buf:
    __slots__ = ("name", "r", "w", "dslot")

    def __init__(self, name):
        self.name = name
        self.r = {}
        self.w = {}
        self.dslot = None


class SemSlot:
    __slots__ = ("sem", "count")

    def __init__(self, sem):
        self.sem = sem
        self.count = 0


def _merge(d, tok):
    sem, val = tok
    k = id(sem)
    if k not in d or d[k][1] < val:
        d[k] = (sem, val)


class EngQ:
    def __init__(self, ctx, eng, name):
        self.eng = eng
        self.name = name
        self.sem = ctx.es.enter_context(ctx.nc.semaphore("q_" + name))
        self.count = 0
        self.waited = {}
        self.ninst = 0

    def wait_tokens(self, toks):
        for sem, val in toks:
            if sem is self.sem and self.name == "tensor":
                continue
            k = id(sem)
            if self.waited.get(k, 0) < val:
                self.eng.wait_ge(sem, val)
                self.waited[k] = val
                self.ninst += 1


class Ctx:
    def __init__(self, nc):
        self.nc = nc
        self.es = ExitStack()
        self.q = {n: EngQ(self, getattr(nc, n), n) for n in ("tensor", "vector", "scalar", "gpsimd", "sync")}
        self.free_slots = []
        self.nslots = 0
        self.bufs = []
        self.pes = None
        self.uid = 0
        self.stq = "sync"

    def begin_phase(self):
        self.pes = ExitStack()
        self.phase_bufs = []

    def end_phase(self):
        self.barrier()
        for b in self.phase_bufs:
            if b.dslot is not None:
                self.free_slots.append(b.dslot)
                b.dslot = None
        self.pes.close()
        self.pes = None

    def sbuf(self, name, shape, dtype):
        self.uid += 1
        return self.pes.enter_context(self.nc.sbuf_tensor(f"{name}_{self.uid}", list(shape), dtype))

    def psum(self, name, shape, dtype=F32):
        self.uid += 1
        return self.pes.enter_context(self.nc.psum_tensor(f"{name}_{self.uid}", list(shape), dtype))

    def buf(self, name, persistent=False):
        b = Buf(name)
        self.bufs.append(b)
        if not persistent and self.pes is not None:
            self.phase_bufs.append(b)
        return b

    def bufs_n(self, name, n):
        return [self.buf(f"{name}{i}") for i in range(n)]

    def _slot(self, b):
        if b.dslot is None:
            if self.free_slots:
                b.dslot = self.free_slots.pop()
            else:
                self.nslots += 1
                b.dslot = SemSlot(self.es.enter_context(self.nc.semaphore(f"d{self.nslots}")))
        return b.dslot

    def _deps(self, reads, writes):
        d = {}
        for b in reads:
            for t in b.w.values():
                _merge(d, t)
        for b in writes:
            for t in b.w.values():
                _merge(d, t)
            for t in b.r.values():
                _merge(d, t)
        return list(d.values())

    def op(self, qname, fn, reads=(), writes=(), signal=True):
        q = self.q[qname]
        q.wait_tokens(self._deps(reads, writes))
        ins = fn(q.eng)
        q.ninst += 1
        if signal:
            q.count += 1
            ins.then_inc(q.sem, 1)
            tok = (q.sem, q.count)
        else:
            tok = (q.sem, q.count + 1)
        for b in reads:
            _merge(b.r, tok)
        for b in writes:
            _merge(b.w, tok)
        return tok

    def dma(self, qname, out, in_, owner, reads=(), writes=()):
        q = self.q[qname]
        slot = self._slot(owner)
        q.wait_tokens(self._deps(reads, writes))
        ins = q.eng.dma_start(out=out, in_=in_)
        q.ninst += 1
        slot.count += 16
        ins.then_inc(slot.sem, 16)
        tok = (slot.sem, slot.count)
        for b in reads:
            _merge(b.r, tok)
        for b in writes:
            _merge(b.w, tok)
        return tok

    def _all_tokens(self):
        d = {}
        for b in self.bufs:
            for t in b.w.values():
                _merge(d, t)
            for t in b.r.values():
                _merge(d, t)
        return list(d.values())

    def barrier(self):
        toks = self._all_tokens()
        for q in self.q.values():
            q.wait_tokens(toks)
        for b in self.bufs:
            b.r.clear()
            b.w.clear()
        self.bufs = [b for b in self.bufs if b not in self.phase_bufs] if self.pes is not None else self.bufs

    def close(self):
        self.es.close()


class Res:
    def __init__(self, c, wring=True, nstage=4):
        self.c = c
        self.ps = [c.psum(f"ps{i}", [128, 512], F32) for i in range(8)]
        self.b_ps = c.bufs_n("ps", 8)
        self.nacc = 0
        if wring:
            self.wt = [c.sbuf(f"wt{i}", [128, KTC, CW], BF16) for i in range(NWS)]
            self.b_wt = c.bufs_n("wt", NWS)
            self.nw = 0
        self.st32 = [c.sbuf(f"st32_{i}", [128, 512], F32) for i in range(nstage)]
        self.b_st32 = c.bufs_n("st32_", nstage)
        self.st16 = [c.sbuf(f"st16_{i}", [128, 512], BF16) for i in range(nstage)]
        self.b_st16 = c.bufs_n("st16_", nstage)
        self.nst = 0
        self.nev = 0

    def stage(self, dtype):
        i = self.nst % len(self.st32)
        self.nst += 1
        if dtype == F32:
            return self.st32[i], self.b_st32[i]
        return self.st16[i], self.b_st16[i]

    def evq(self):
        self.nev += 1
        return "scalar" if self.nev % 2 else "vector"


def copy_op(qn, out, in_):
    if qn == "scalar":
        return lambda e: e.activation(out, in_, AF.Copy)
    return lambda e: e.tensor_copy(out, in_)


def gemm(c, R, act, b_act, kt_n, mb, w_ap, col0, ncols, orient, epi, tok_cols=None):
    nkc = (kt_n + KTC - 1) // KTC
    if tok_cols is None:
        step = 512 if orient == "A" else 128
        tok_cols = [(s, min(step, mb - s)) for s in range(0, mb, step)]
    for c0 in range(col0, col0 + ncols, CW):
        cw = min(CW, col0 + ncols - c0)
        if orient == "A":
            accs = [(fi, ti) for fi in range((cw + 127) // 128) for ti in range(len(tok_cols))]
        else:
            accs = [(0, ti) for ti in range(len(tok_cols))]
        for a0 in range(0, len(accs), 4):
            grp = accs[a0:a0 + 4]
            base = 4 * (R.nacc % 2)
            R.nacc += 1
            for kc in range(nkc):
                k0 = kc * KTC
                kn = min(KTC, kt_n - k0)
                s = R.nw % NWS
                R.nw += 1
                src = w_ap[k0 * 128:(k0 + kn) * 128, c0:c0 + cw].rearrange("(kt p) n -> p kt n", p=128)
                c.dma("gpsimd", R.wt[s][:, 0:kn, 0:cw], src, R.b_wt[s], writes=[R.b_wt[s]])
                for gi, (fi, ti) in enumerate(grp):
                    ts, tn = tok_cols[ti]
                    bank = base + gi
                    for kk in range(kn):
                        kt = k0 + kk
                        last = (kt == kt_n - 1)
                        if orient == "A":
                            fn_ = min(128, cw - fi * 128)
                            c.op("tensor", lambda e: e.matmul(R.ps[bank][0:fn_, 0:tn], R.wt[s][:, kk, fi * 128:fi * 128 + fn_],
                                                              act[:, kt, ts:ts + tn], start=(kt == 0), stop=last),
                                 reads=[b_act, R.b_wt[s]], writes=[R.b_ps[bank]], signal=(last or kk == kn - 1))
                        else:
                            c.op("tensor", lambda e: e.matmul(R.ps[bank][0:tn, 0:cw], act[:, kt, ts:ts + tn],
                                                              R.wt[s][:, kk, 0:cw], start=(kt == 0), stop=last),
                                 reads=[b_act, R.b_wt[s]], writes=[R.b_ps[bank]], signal=(last or kk == kn - 1))
            for gi, (fi, ti) in enumerate(grp):
                ts, tn = tok_cols[ti]
                bank = base + gi
                if orient == "A":
                    fn_ = min(128, cw - fi * 128)
                    epi(R.ps[bank][0:fn_, 0:tn], R.b_ps[bank], ti, ts, tn, c0 + fi * 128, fn_)
                else:
                    epi(R.ps[bank][0:tn, 0:cw], R.b_ps[bank], ti, ts, tn, c0, cw)


def load_block(c, dst, b_dst, src, t0, mb, kt_n):
    for k0 in range(0, kt_n, 8):
        kn = min(8, kt_n - k0)
        c.dma("sync", dst[:, k0:k0 + kn, 0:mb],
              src[k0 * 128:(k0 + kn) * 128, t0:t0 + mb].rearrange("(kt p) t -> p kt t", p=128),
              b_dst, writes=[b_dst])


class TransposeStore:
    def __init__(self, c, R, ident, b_ident, dst):
        self.c, self.R, self.ident, self.b_ident, self.dst = c, R, ident, b_ident, dst
        self.stg = [c.sbuf(f"tstg{i}", [128, KT, 512], BF16) for i in range(2)]
        self.b_stg = c.bufs_n("tstg", 2)
        self.ng = 0

    def tile(self, xb, b_xb, tt, ntiles):
        c, R = self.c, self.R
        grp = tt // 4
        s = grp % 2
        j4 = tt % 4
        for g in range(KT // 8):
            bank = self.ng % 8
            self.ng += 1
            pst = R.ps[bank][:].bitcast(BF16)
            for j in range(8):
                kt = g * 8 + j
                c.op("tensor", lambda e: e.transpose(pst[:, j * 128:(j + 1) * 128], xb[:, kt * 128:(kt + 1) * 128], self.ident[:]),
                     reads=[b_xb, self.b_ident], writes=[R.b_ps[bank]], signal=(j == 7))
            qn = R.evq()
            c.op(qn, copy_op(qn, self.stg[s][:, g * 8:(g + 1) * 8, j4 * 128:(j4 + 1) * 128],
                             pst.rearrange("p (a b) -> p a b", a=8)),
                 reads=[R.b_ps[bank]], writes=[self.b_stg[s]])
        if j4 == 3 or tt == ntiles - 1:
            ncol = (j4 + 1) * 128
            t0 = grp * 512
            for k0 in range(0, KT, 8):
                c.dma("gpsimd", self.dst["ap"][k0 * 128:(k0 + 8) * 128, t0:t0 + ncol].rearrange("(kt p) t -> p kt t", p=128),
                      self.stg[s][:, k0:k0 + 8, 0:ncol], self.b_stg[s], reads=[self.b_stg[s]], writes=[self.dst["buf"]])


def rms_rstd(c, ss, b_ss, n):
    c.op("vector", lambda e: e.tensor_scalar(ss, ss, 1.0 / n, EPS, ALU.mult, ALU.add), reads=[b_ss], writes=[b_ss])
    c.op("scalar", lambda e: e.activation(ss, ss, AF.Sqrt), reads=[b_ss], writes=[b_ss])
    c.op("vector", lambda e: e.reciprocal(ss, ss), reads=[b_ss], writes=[b_ss])


def build_program(TOK, MB=512):
    assert TOK % MB == 0 and MB == 512
    NB = TOK // MB
    NT = TOK // 128
    NCH = TOK // L
    nc = bass.Bass("TRN2", target_bir_lowering=False)

    def din(name, shape, dt=F32):
        return nc.dram_tensor(name, list(shape), dt, kind="ExternalInput").ap()

    x = din("x", [TOK, D])
    w_in = din("w_in", [D, IN_W])
    w_mo = din("w_mo", [D, D])
    w_fo = din("w_fo", [D, D])
    w_out = din("w_out", [D, D])
    w_up = din("w_up", [D, 2 * DFF])
    w_down = din("w_down", [DFF, D])
    g_mix = din("g_mix", [128, D])
    g_ffn = din("g_ffn", [128, D])
    g_fin = din("g_fin", [128, D])
    g_mh = din("g_mh", [128, D])
    bgate_d = din("bgate", [64, 32])
    bm_d = din("bm", [128, 64])
    cw_d = din("cw", [128, 3, 2 * FT])
    cb_d = din("cb", [128, 2 * FT])
    ident_d = din("ident", [128, 128])
    tri_d = din("tri", [64, 2, 64])
    mask_d = din("mask", [64, 2, 64])
    flag_d = din("flag", [128, 2])
    dftc_d = din("dftc", [TOK, TOK], BF16)
    dfts_d = din("dfts", [TOK, TOK], BF16)
    cc_d = din("cc", [1024, 1024], BF16)
    scn_d = din("scn", [1024, 1024], BF16)
    out = nc.dram_tensor("out", [TOK, D], F32, kind="ExternalOutput").ap()

    c = Ctx(nc)

    def scratch(name, shape, dt):
        return {"ap": nc.dram_tensor(name, list(shape), dt).ap(), "buf": c.buf(name, persistent=True)}

    xnT = scratch("s_xnT", [D, TOK], BF16)
    qT = scratch("s_qT", [2048, TOK], BF16)
    kT = scratch("s_kT", [2048, TOK], BF16)
    kk = scratch("s_kk", [TOK, 2048], BF16)
    vv = scratch("s_vv", [TOK, D], BF16)
    og = scratch("s_og", [TOK, D], F32)
    gif = scratch("s_gif", [TOK, 32], F32)
    frX = scratch("s_frX", [TOK, D], BF16)
    gmT = scratch("s_gmT", [D, TOK], F32)
    gfT = scratch("s_gfT", [D, TOK], F32)
    hdir = [scratch("s_hF", [TOK, D], F32), scratch("s_hB", [TOK, D], F32)]
    hgT = scratch("s_hgT", [D, TOK], BF16)
    zT = scratch("s_zT", [D, TOK], BF16)
    mgT = scratch("s_mgT", [D, TOK], BF16)
    x1 = scratch("s_x1", [TOK, D], F32)
    xn2T = scratch("s_xn2T", [D, TOK], BF16)
    outb = {"ap": out, "buf": c.buf("out", persistent=True)}
    b_in = c.buf("inputs", persistent=True)

    def load_const(name, shape, src, dt=F32, q="sync"):
        t = c.sbuf(name, shape, dt)
        b = c.buf(name)
        c.dma(q, t[:], src, b, writes=[b])
        return t, b

    def load_ident():
        idf, b_idf = load_const("identf", [128, 128], ident_d)
        idb = c.sbuf("identb", [128, 128], BF16)
        b_idb = c.buf("identb")
        c.op("vector", lambda e: e.tensor_copy(idb[:], idf[:]), reads=[b_idf], writes=[b_idb])
        return idb, b_idb

    def phase_norm_T(src, gsrc, dst):
        c.begin_phase()
        R = Res(c, wring=False, nstage=1)
        idb, b_idb = load_ident()
        gbc, b_g = load_const("gbc", [128, D], gsrc)
        TS = TransposeStore(c, R, idb, b_idb, dst)
        xt = [c.sbuf(f"xt{i}", [128, D], F32) for i in range(2)]
        b_xt = c.bufs_n("xt", 2)
        xb = [c.sbuf(f"xb{i}", [128, D], BF16) for i in range(2)]
        b_xb = c.bufs_n("xb", 2)
        ss = [c.sbuf(f"ss{i}", [128, 1], F32) for i in range(2)]
        b_ss = c.bufs_n("ss", 2)
        for tt in range(NT):
            s = tt % 2
            c.dma("sync", xt[s][:], src["ap"][tt * 128:(tt + 1) * 128, :], b_xt[s], reads=[src["buf"]], writes=[b_xt[s]])
            c.op("scalar", lambda e: e.activation(xb[s][:], xt[s][:], AF.Square, accum_out=ss[s][:]),
                 reads=[b_xt[s]], writes=[b_xb[s], b_ss[s]])
            rms_rstd(c, ss[s][:], b_ss[s], D)
            c.op("vector", lambda e: e.scalar_tensor_tensor(xb[s][:], xt[s][:], ss[s][:, 0:1], gbc[:], ALU.mult, ALU.mult),
                 reads=[b_xt[s], b_ss[s], b_g], writes=[b_xb[s]])
            TS.tile(xb[s], b_xb[s], tt, NT)
        c.end_phase()

    def phase_A():
        c.begin_phase()
        R = Res(c)
        act = c.sbuf("actA", [128, KT, MB], BF16)
        b_act = c.buf("actA")

        def store_epi(dst, dt, orient):
            def epi(ps, b_ps, ti, ts, tn, c0, ncol, dst=dst, dt=dt):
                stg, b_stg = R.stage(dt)
                qn = R.evq()
                if orient == "A":
                    sv = stg[0:ncol, 0:tn]
                    dv = dst["ap"][c0 - dst["c0"]:c0 - dst["c0"] + ncol, t0 + ts:t0 + ts + tn]
                else:
                    sv = stg[0:tn, 0:ncol]
                    dv = dst["ap"][t0 + ts:t0 + ts + tn, c0 - dst["c0"]:c0 - dst["c0"] + ncol]
                c.op(qn, copy_op(qn, sv, ps), reads=[b_ps], writes=[b_stg])
                c.dma("sync", dv, sv, b_stg, reads=[b_stg], writes=[dst["buf"]])
            return epi

        segs = [
            (OFF_Q, 2048, "A", qT, BF16), (OFF_K, 2048, "A", kT, BF16), (OFF_K, 2048, "B", kk, BF16),
            (OFF_V, 4096, "B", vv, BF16), (OFF_O, 4096, "B", og, F32), (OFF_I, 32, "B", gif, F32),
            (OFF_FR, 4096, "B", frX, BF16), (OFF_GM, 4096, "A", gmT, F32), (OFF_GF, 4096, "A", gfT, F32),
        ]
        for blk in range(NB):
            t0 = blk * MB
            load_block(c, act, b_act, xnT["ap"], t0, MB, KT)
            for (c0, n, orient, dst, dt) in segs:
                dst["c0"] = c0
                gemm(c, R, act, b_act, KT, MB, w_in, c0, n, orient, store_epi(dst, dt, orient))
        c.end_phase()

    def phase_B():
        c.begin_phase()
        ps = [c.psum(f"psB{i}", [128, 512], F32) for i in range(8)]
        b_psS = c.bufs_n("psS", H)
        b_psG = c.buf("psG")
        b_psD = c.buf("psD")
        b_psn = c.buf("psn")
        b_psN = c.bufs_n("psN", 2)
        b_psC = c.bufs_n("psC", 4)
        tri, b_tri = load_const("tri", [64, 2, 64], tri_d)
        mask, b_mask = load_const("mask", [64, 2, 64], mask_d)
        bgate, b_bgate = load_const("bgate", [64, 32], bgate_d)
        flag, b_flag = load_const("flag", [128, 2], flag_d)
        onesn = c.sbuf("onesn", [64, 128], F32)
        ones16 = c.sbuf("ones16", [64, 1], BF16)
        b_ones = c.buf("ones")
        c.op("vector", lambda e: e.memset(onesn[:], -1.0), writes=[b_ones])
        c.op("vector", lambda e: e.memset(ones16[:], 1.0), writes=[b_ones])
        C32 = c.sbuf("C32", [128, H, 2, DV], F32)
        Cb = c.sbuf("Cb", [128, H, 2, DV], BF16)
        n32 = c.sbuf("n32", [128, 2 * H], F32)
        nb = c.sbuf("nb", [128, 2 * H], BF16)
        b_C = [c.buf(f"C{h}") for h in range(H)]
        b_Cb = [c.buf(f"Cb{h}") for h in range(H)]
        b_n = c.buf("n32")
        b_nb = c.buf("nb")
        qs = [c.sbuf(f"qs{i}", [128, 16, 128], BF16) for i in range(2)]
        ks = [c.sbuf(f"ks{i}", [128, 16, 128], BF16) for i in range(2)]
        kks = [c.sbuf(f"kks{i}", [64, 2, 2048], BF16) for i in range(2)]
        vvs = [c.sbuf(f"vvs{i}", [64, 2, D], BF16) for i in range(2)]
        gis = [c.sbuf(f"gis{i}", [64, 2, 32], F32) for i in range(2)]
        b_in_s = c.bufs_n("mls_in", 2)
        hbuf = [c.sbuf(f"hbuf{i}", [64, D], F32) for i in range(2)]
        b_hbuf = c.bufs_n("hbuf", 2)
        vt = [c.sbuf(f"vt{i}", [64, DV], BF16) for i in range(2)]
        b_vt = c.bufs_n("vt", 2)
        PT = [c.sbuf(f"PT{i}", [64, H, 64], BF16) for i in range(2)]
        b_PT = c.bufs_n("PT", 2)
        gt = [c.sbuf(f"gt{i}", [128, 80], F32) for i in range(2)]
        b_gt = c.bufs_n("gt", 2)
        ab = [c.sbuf(f"ab{i}", [64, H], BF16) for i in range(2)]
        nvt = 0
        for d in range(2):
            c.op("vector", lambda e: e.memset(C32[:], 0.0), writes=b_C)
            c.op("vector", lambda e: e.memset(Cb[:], 0.0), writes=b_Cb)
            c.op("vector", lambda e: e.memset(n32[:], 0.0), writes=[b_n])
            c.op("vector", lambda e: e.memset(nb[:], 0.0), writes=[b_nb])
            order = list(range(NCH)) if d == 0 else list(range(NCH - 1, -1, -1))
            reset_at = NCH // 2 if d == 0 else NCH // 2 - 1
            def load_sc(step_):
                if step_ >= NCH:
                    return
                sc = order[step_] // 2
                s = (step_ // 2) % 2
                if True:
                    t0 = sc * 128
                    bi = b_in_s[s]
                    c.dma("sync", qs[s][:], qT["ap"][:, t0:t0 + 128].rearrange("(kt p) t -> p kt t", p=128), bi, reads=[qT["buf"]], writes=[bi])
                    c.dma("sync", ks[s][:], kT["ap"][:, t0:t0 + 128].rearrange("(kt p) t -> p kt t", p=128), bi, reads=[kT["buf"]], writes=[bi])
                    c.dma("sync", kks[s][:], kk["ap"][t0:t0 + 128, :].rearrange("(c p) n -> p c n", p=64), bi, reads=[kk["buf"]], writes=[bi])
                    c.dma("sync", vvs[s][:], vv["ap"][t0:t0 + 128, :].rearrange("(c p) n -> p c n", p=64), bi, reads=[vv["buf"]], writes=[bi])
                    c.dma("sync", gis[s][:], gif["ap"][t0:t0 + 128, :].rearrange("(c p) n -> p c n", p=64), bi, reads=[gif["buf"]], writes=[bi])

            load_sc(0)
            for step, ci in enumerate(order):
                lc = ci % 2
                s = (step // 2) % 2
                if step % 2 == 0:
                    load_sc(step + 2)
                bi = b_in_s[s]
                cs = slice(lc * 64, lc * 64 + 64)
                if ci == reset_at:
                    c.op("vector", lambda e: e.tensor_scalar(C32[:], C32[:], flag[:, 0:1], None, ALU.mult), reads=[b_flag], writes=b_C)
                    c.op("vector", lambda e: e.tensor_scalar(Cb[:], Cb[:], flag[:, 0:1], None, ALU.mult), reads=[b_flag], writes=b_Cb)
                    c.op("vector", lambda e: e.tensor_scalar(n32[:], n32[:], flag[:, 0:1], None, ALU.mult), reads=[b_flag], writes=[b_n])
                    c.op("vector", lambda e: e.tensor_scalar(nb[:], nb[:], flag[:, 0:1], None, ALU.mult), reads=[b_flag], writes=[b_nb])
                g_ = gt[step % 2]
                bg = b_gt[step % 2]
                a_b = ab[step % 2]
                zt, sp, igt, wt_, et, eg, at, dn, rr = (g_[0:64, 0:8], g_[0:64, 8:16], g_[0:64, 16:24], g_[0:64, 24:32],
                                                        g_[0:64, 32:40], g_[:, 40:48], g_[0:64, 48:56], g_[0:64, 56:64], g_[0:64, 64:72])
                c.op("vector", lambda e: e.tensor_tensor(zt, gis[s][:, lc, 16 + 8 * d:24 + 8 * d], bgate[:, 16 + 8 * d:24 + 8 * d], ALU.add),
                     reads=[bi, b_bgate], writes=[bg])
                c.op("scalar", lambda e: e.activation(sp, zt, AF.Exp, scale=-1.0), reads=[bg], writes=[bg])
                c.op("scalar", lambda e: e.activation(sp, sp, AF.Ln, bias=1.0), reads=[bg], writes=[bg])
                c.op("tensor", lambda e: e.matmul(ps[1][0:64, 0:8], tri[:, d, :], sp, start=True, stop=True),
                     reads=[b_tri, bg], writes=[b_psG])
                c.op("tensor", lambda e: e.matmul(ps[1][:, 8:16], onesn[:], sp, start=True, stop=True),
                     reads=[b_ones, bg], writes=[b_psG])
                c.op("vector", lambda e: e.tensor_tensor(igt, gis[s][:, lc, 8 * d:8 * d + 8], bgate[:, 8 * d:8 * d + 8], ALU.add),
                     reads=[bi, b_bgate], writes=[bg])
                c.op("vector", lambda e: e.tensor_tensor(igt, igt, ps[1][0:64, 0:8], ALU.subtract), reads=[b_psG, bg], writes=[bg])
                c.op("scalar", lambda e: e.activation(wt_, igt, AF.Exp, bias=-math.log(16.0)), reads=[bg], writes=[bg])
                c.op("scalar", lambda e: e.activation(et, ps[1][0:64, 0:8], AF.Exp), reads=[b_psG], writes=[bg])
                c.op("scalar", lambda e: e.activation(eg, ps[1][:, 8:16], AF.Exp), reads=[b_psG], writes=[bg])
                c.op("vector", lambda e: e.tensor_tensor(at, wt_, g_[0:64, 40:48], ALU.mult), reads=[bg], writes=[bg])
                c.op("vector", lambda e: e.tensor_copy(a_b[:], at), reads=[bg], writes=[bg])
                pt = PT[step % 2]
                bpt = b_PT[step % 2]
                for h in range(H):
                    for dt in range(2):
                        c.op("tensor", lambda e: e.matmul(ps[0][0:64, h * 64:(h + 1) * 64], ks[s][:, 2 * h + dt, cs], qs[s][:, 2 * h + dt, cs],
                                                          start=(dt == 0), stop=(dt == 1)),
                             reads=[bi], writes=[b_psS[h]], signal=(dt == 1))
                    c.op("vector", lambda e: e.scalar_tensor_tensor(pt[:, h, :], ps[0][0:64, h * 64:(h + 1) * 64], wt_[:, h:h + 1], mask[:, d, :],
                                                                    ALU.mult, ALU.mult),
                         reads=[b_psS[h], bg, b_mask], writes=[bpt])
                    c.op("tensor", lambda e: e.matmul(ps[1][0:64, 16 + h:17 + h], pt[:, h, :], ones16[:], start=True, stop=False),
                         reads=[bpt, b_ones], writes=[b_psD], signal=False)
                    for dt in range(2):
                        c.op("tensor", lambda e: e.matmul(ps[1][0:64, 16 + h:17 + h], qs[s][:, 2 * h + dt, cs], nb[:, 2 * h + dt:2 * h + dt + 1],
                                                          start=False, stop=(dt == 1)),
                             reads=[bi, b_nb], writes=[b_psD], signal=(dt == 1))
                c.op("vector", lambda e: e.tensor_tensor(dn, ps[1][0:64, 16:24], et, ALU.mult), reads=[b_psD, bg], writes=[bg])
                dn2 = g_[0:64, 72:80]
                c.op("vector", lambda e: e.tensor_scalar(dn2, dn, -1.0, None, ALU.mult), reads=[bg], writes=[bg])
                c.op("vector", lambda e: e.tensor_tensor(dn, dn, dn2, ALU.max), reads=[bg], writes=[bg])
                c.op("vector", lambda e: e.tensor_scalar(dn, dn, 1.0, None, ALU.max), reads=[bg], writes=[bg])
                c.op("vector", lambda e: e.reciprocal(dn, dn), reads=[bg], writes=[bg])
                c.op("vector", lambda e: e.tensor_tensor(rr, dn, et, ALU.mult), reads=[bg], writes=[bg])
                hb = hbuf[step % 2]
                bhb = b_hbuf[step % 2]
                for h in range(H):
                    pn = 2 + (h % 2)
                    c.op("tensor", lambda e: e.matmul(ps[pn][0:64, :], pt[:, h, :], vvs[s][:, lc, h * DV:(h + 1) * DV], start=True, stop=False),
                         reads=[bpt, bi], writes=[b_psN[h % 2]], signal=False)
                    for dt in range(2):
                        c.op("tensor", lambda e: e.matmul(ps[pn][0:64, :], qs[s][:, 2 * h + dt, cs], Cb[:, h, dt, :], start=False, stop=(dt == 1)),
                             reads=[bi, b_Cb[h]], writes=[b_psN[h % 2]], signal=(dt == 1))
                    c.op("scalar", lambda e: e.activation(hb[:, h * DV:(h + 1) * DV], ps[pn][0:64, :], AF.Copy, scale=rr[:, h:h + 1]),
                         reads=[b_psN[h % 2], bg], writes=[bhb])
                    v_ = vt[nvt % 2]
                    bv = b_vt[nvt % 2]
                    nvt += 1
                    c.op("scalar", lambda e: e.activation(v_[:], vvs[s][:, lc, h * DV:(h + 1) * DV], AF.Copy, scale=at[:, h:h + 1]),
                         reads=[bi, bg], writes=[bv])
                    for dt in range(2):
                        pc = 4 + 2 * (h % 2) + dt
                        kslice = kks[s][:, lc, h * DQK + dt * 128:h * DQK + dt * 128 + 128]
                        c.op("tensor", lambda e: e.matmul(ps[pc][:, :], kslice, v_[:], start=True, stop=True),
                             reads=[bi, bv], writes=[b_psC[pc - 4]])
                        c.op("tensor", lambda e: e.matmul(ps[1][:, 24 + 2 * h + dt:25 + 2 * h + dt], kslice, a_b[:, h:h + 1], start=True, stop=True),
                             reads=[bi, bg], writes=[b_psn])
                        c.op("vector", lambda e: e.scalar_tensor_tensor(C32[:, h, dt, :], C32[:, h, dt, :], eg[:, h:h + 1], ps[pc][:, :],
                                                                        ALU.mult, ALU.add),
                             reads=[bg, b_psC[pc - 4]], writes=[b_C[h]])
                        cq = "scalar"
                        c.op(cq, copy_op(cq, Cb[:, h, dt, :], C32[:, h, dt, :]), reads=[b_C[h]], writes=[b_Cb[h]])
                n3 = n32[:].rearrange("p (h t) -> p h t", t=2)
                p3 = ps[1][:, 24:40].rearrange("p (h t) -> p h t", t=2)
                for dt in range(2):
                    c.op("vector", lambda e: e.tensor_tensor(n3[:, :, dt], n3[:, :, dt], eg, ALU.mult), reads=[bg], writes=[b_n])
                    c.op("vector", lambda e: e.tensor_tensor(n3[:, :, dt], n3[:, :, dt], p3[:, :, dt], ALU.add), reads=[b_psn], writes=[b_n])
                c.op("vector", lambda e: e.tensor_copy(nb[:], n32[:]), reads=[b_n], writes=[b_nb])
                c.dma("gpsimd", hdir[d]["ap"][ci * 64:(ci + 1) * 64, :], hb[:], bhb, reads=[bhb], writes=[hdir[d]["buf"]])
        c.end_phase()

    def phase_C0():
        c.begin_phase()
        R = Res(c, wring=False, nstage=1)
        idb, b_idb = load_ident()
        gbc, b_g = load_const("gmh", [128, D], g_mh)
        TS = TransposeStore(c, R, idb, b_idb, hgT)
        ha = [c.sbuf(f"ha{i}", [128, D], F32) for i in range(2)]
        hb_ = [c.sbuf(f"hb{i}", [128, D], F32) for i in range(2)]
        ho = [c.sbuf(f"ho{i}", [128, D], F32) for i in range(2)]
        b_ha = c.bufs_n("ha", 2)
        b_hb = c.bufs_n("hb", 2)
        b_ho = c.bufs_n("ho", 2)
        xb = [c.sbuf(f"hx{i}", [128, D], BF16) for i in range(2)]
        b_xb = c.bufs_n("hx", 2)
        ss = [c.sbuf(f"hss{i}", [128, H], F32) for i in range(2)]
        b_ss = c.bufs_n("hss", 2)
        for tt in range(NT):
            s = tt % 2
            rows = slice(tt * 128, (tt + 1) * 128)
            c.dma("sync", ha[s][:], hdir[0]["ap"][rows, :], b_ha[s], reads=[hdir[0]["buf"]], writes=[b_ha[s]])
            c.dma("sync", hb_[s][:], hdir[1]["ap"][rows, :], b_hb[s], reads=[hdir[1]["buf"]], writes=[b_hb[s]])
            c.dma("sync", ho[s][:], og["ap"][rows, :], b_ho[s], reads=[og["buf"]], writes=[b_ho[s]])
            c.op("vector", lambda e: e.tensor_tensor(ha[s][:], ha[s][:], hb_[s][:], ALU.add), reads=[b_hb[s]], writes=[b_ha[s]])
            for h in range(H):
                c.op("scalar", lambda e: e.activation(hb_[s][:, h * DV:(h + 1) * DV], ha[s][:, h * DV:(h + 1) * DV], AF.Square,
                                                      accum_out=ss[s][:, h:h + 1]),
                     reads=[b_ha[s]], writes=[b_hb[s], b_ss[s]])
            rms_rstd(c, ss[s][:], b_ss[s], DV)
            c.op("scalar", lambda e: e.activation(ho[s][:], ho[s][:], AF.Sigmoid), reads=[], writes=[b_ho[s]])
            for h in range(H):
                c.op("vector", lambda e: e.tensor_scalar(ha[s][:, h * DV:(h + 1) * DV], ha[s][:, h * DV:(h + 1) * DV], ss[s][:, h:h + 1], None, ALU.mult),
                     reads=[b_ss[s]], writes=[b_ha[s]])
            c.op("vector", lambda e: e.tensor_tensor(ho[s][:], ho[s][:], gbc[:], ALU.mult), reads=[b_g], writes=[b_ho[s]])
            c.op("vector", lambda e: e.tensor_tensor(xb[s][:], ha[s][:], ho[s][:], ALU.mult), reads=[b_ha[s], b_ho[s]], writes=[b_xb[s]])
            TS.tile(xb[s], b_xb[s], tt, NT)
        c.end_phase()

    def phase_D():
        c.begin_phase()
        R = Res(c, wring=False)
        ST = TOK // 128
        TB = 256
        Xg = c.sbuf("Xg", [128, ST, 1024], BF16)
        b_Xg = c.buf("Xg")
        dcs = [c.sbuf(f"dc{i}", [128, ST, TB], BF16) for i in range(2)]
        dss = [c.sbuf(f"ds{i}", [128, ST, TB], BF16) for i in range(2)]
        b_dft = c.bufs_n("dft", 2)
        ccs, b_cc = load_const("ccs", [128, 8, 1024], cc_d.rearrange("(kt p) n -> p kt n", p=128), BF16)
        scs, b_sc = load_const("scs", [128, 8, 1024], scn_d.rearrange("(kt p) n -> p kt n", p=128), BF16)
        A1 = [c.sbuf(f"A1_{i}", [128, 8, TB], BF16) for i in range(2)]
        A2 = [c.sbuf(f"A2_{i}", [128, 8, TB], BF16) for i in range(2)]
        b_A = c.bufs_n("Aseq", 2)
        nit = 0
        nbk = 0
        NTB = TOK // TB

        def load_dft(i):
            if i >= 4 * NTB:
                return
            s = i % 2
            tb = i % NTB
            for s0 in range(0, ST, 8):
                c.dma("sync", dcs[s][:, s0:s0 + 8, :], dftc_d[s0 * 128:(s0 + 8) * 128, tb * TB:(tb + 1) * TB].rearrange("(st p) n -> p st n", p=128),
                      b_dft[s], writes=[b_dft[s]])
                c.dma("sync", dss[s][:, s0:s0 + 8, :], dfts_d[s0 * 128:(s0 + 8) * 128, tb * TB:(tb + 1) * TB].rearrange("(st p) n -> p st n", p=128),
                      b_dft[s], writes=[b_dft[s]])

        for g in range(4):
            for s0 in range(0, ST, 8):
                c.dma("sync", Xg[:, s0:s0 + 8, :], frX["ap"][s0 * 128:(s0 + 8) * 128, g * 1024:(g + 1) * 1024].rearrange("(st p) n -> p st n", p=128),
                      b_Xg, reads=[frX["buf"]], writes=[b_Xg])
            for tb in range(TOK // TB):
                s = nit % 2
                if nit == 0:
                    load_dft(0)
                load_dft(nit + 1)
                nit += 1
                for ct in range(8):
                    b1 = nbk % 8
                    b2 = (nbk + 1) % 8
                    nbk += 2
                    for st in range(ST):
                        c.op("tensor", lambda e: e.matmul(R.ps[b1][:, 0:TB], Xg[:, st, ct * 128:(ct + 1) * 128], dcs[s][:, st, :],
                                                          start=(st == 0), stop=(st == ST - 1)),
                             reads=[b_Xg, b_dft[s]], writes=[R.b_ps[b1]], signal=(st == ST - 1))
                        c.op("tensor", lambda e: e.matmul(R.ps[b2][:, 0:TB], Xg[:, st, ct * 128:(ct + 1) * 128], dss[s][:, st, :],
                                                          start=(st == 0), stop=(st == ST - 1)),
                             reads=[b_Xg, b_dft[s]], writes=[R.b_ps[b2]], signal=(st == ST - 1))
                    c.op("scalar", copy_op("scalar", A1[s][:, ct, :], R.ps[b1][:, 0:TB]), reads=[R.b_ps[b1]], writes=[b_A[s]])
                    c.op("vector", copy_op("vector", A2[s][:, ct, :], R.ps[b2][:, 0:TB]), reads=[R.b_ps[b2]], writes=[b_A[s]])
                for c2 in range(8):
                    bk = nbk % 8
                    nbk += 1
                    for ct in range(8):
                        c.op("tensor", lambda e: e.matmul(R.ps[bk][:, 0:TB], ccs[:, ct, c2 * 128:(c2 + 1) * 128], A1[s][:, ct, :],
                                                          start=(ct == 0), stop=False),
                             reads=[b_cc, b_A[s]], writes=[R.b_ps[bk]], signal=False)
                        c.op("tensor", lambda e: e.matmul(R.ps[bk][:, 0:TB], scs[:, ct, c2 * 128:(c2 + 1) * 128], A2[s][:, ct, :],
                                                          start=False, stop=(ct == 7)),
                             reads=[b_sc, b_A[s]], writes=[R.b_ps[bk]], signal=(ct == 7))
                    stg, b_stg = R.stage(BF16)
                    qn = R.evq()
                    c.op(qn, copy_op(qn, stg[:, 0:TB], R.ps[bk][:, 0:TB]), reads=[R.b_ps[bk]], writes=[b_stg])
                    r0 = g * 1024 + c2 * 128
                    c.dma("gpsimd", zT["ap"][r0:r0 + 128, tb * TB:(tb + 1) * TB], stg[:, 0:TB], b_stg, reads=[b_stg], writes=[zT["buf"]])
        c.end_phase()

    def phase_C():
        c.begin_phase()
        R = Res(c)
        actm = c.sbuf("actm", [128, KT, MB], BF16)
        actf = c.sbuf("actf", [128, KT, MB], BF16)
        b_am = c.buf("actm")
        b_af = c.buf("actf")
        bm, b_bm = load_const("bm", [128, 64], bm_d)
        t1 = [c.sbuf(f"t1_{i}", [128, 512], F32) for i in range(8)]
        b_t1 = c.bufs_n("t1_", 8)
        gl = [c.sbuf(f"gl{i}", [128, 512], F32) for i in range(4)]
        b_gl = c.bufs_n("gl", 4)
        cnt = {"g": 0, "t": 0}
        for blk in range(NB):
            t0 = blk * MB
            load_block(c, actm, b_am, hgT["ap"], t0, MB, KT)
            load_block(c, actf, b_af, zT["ap"], t0, MB, KT)

            def gate(src, boff, c0, ncol, ts, tn):
                i = cnt["g"] % 4
                cnt["g"] += 1
                c.dma("sync", gl[i][0:ncol, 0:tn], src["ap"][c0:c0 + ncol, t0 + ts:t0 + ts + tn], b_gl[i], reads=[src["buf"]], writes=[b_gl[i]])
                ft = boff + c0 // 128
                c.op("scalar", lambda e: e.activation(gl[i][0:ncol, 0:tn], gl[i][0:ncol, 0:tn], AF.Sigmoid, bias=bm[0:ncol, ft:ft + 1]),
                     reads=[b_bm], writes=[b_gl[i]])
                return gl[i], b_gl[i]

            for c0 in range(0, D, CW):
                slots = {}

                def epi1(ps, b_ps, ti, ts, tn, cc0, ncol):
                    g_, bg_ = gate(gmT, 0, cc0, ncol, ts, tn)
                    i = cnt["t"] % 8
                    cnt["t"] += 1
                    slots[(cc0, ti)] = i
                    c.op("vector", lambda e: e.tensor_tensor(t1[i][0:ncol, 0:tn], ps, g_[0:ncol, 0:tn], ALU.mult),
                         reads=[b_ps, bg_], writes=[b_t1[i]])

                def epi2(ps, b_ps, ti, ts, tn, cc0, ncol):
                    g_, bg_ = gate(gfT, 32, cc0, ncol, ts, tn)
                    i = slots[(cc0, ti)]
                    c.op("vector", lambda e: e.tensor_tensor(g_[0:ncol, 0:tn], ps, g_[0:ncol, 0:tn], ALU.mult),
                         reads=[b_ps], writes=[bg_])
                    stg, b_stg = R.stage(BF16)
                    c.op("vector", lambda e: e.tensor_tensor(stg[0:ncol, 0:tn], g_[0:ncol, 0:tn], t1[i][0:ncol, 0:tn], ALU.add),
                         reads=[bg_, b_t1[i]], writes=[b_stg])
                    c.dma("sync", mgT["ap"][cc0:cc0 + ncol, t0 + ts:t0 + ts + tn], stg[0:ncol, 0:tn], b_stg, reads=[b_stg], writes=[mgT["buf"]])

                gemm(c, R, actm, b_am, KT, MB, w_mo, c0, CW, "A", epi1)
                gemm(c, R, actf, b_af, KT, MB, w_fo, c0, CW, "A", epi2)
        c.end_phase()

    def phase_E():
        c.begin_phase()
        R = Res(c)
        act = c.sbuf("actE", [128, KT, MB], BF16)
        b_act = c.buf("actE")
        xl = [c.sbuf(f"xl{i}", [128, 512], F32) for i in range(4)]
        b_xl = c.bufs_n("xl", 4)
        cnt = {"x": 0}
        for blk in range(NB):
            t0 = blk * MB
            load_block(c, act, b_act, mgT["ap"], t0, MB, KT)

            def epi(ps, b_ps, ti, ts, tn, c0, ncol):
                i = cnt["x"] % 4
                cnt["x"] += 1
                rows = slice(t0 + ts, t0 + ts + tn)
                c.dma("sync", xl[i][0:tn, 0:ncol], x[rows, c0:c0 + ncol], b_xl[i], writes=[b_xl[i]])
                c.op("vector", lambda e: e.tensor_tensor(xl[i][0:tn, 0:ncol], ps, xl[i][0:tn, 0:ncol], ALU.add), reads=[b_ps], writes=[b_xl[i]])
                c.dma("sync", x1["ap"][rows, c0:c0 + ncol], xl[i][0:tn, 0:ncol], b_xl[i], reads=[b_xl[i]], writes=[x1["buf"]])

            gemm(c, R, act, b_act, KT, MB, w_out, 0, D, "B", epi)
        c.end_phase()

    def phase_F():
        c.begin_phase()
        R = Res(c, nstage=1)
        act = c.sbuf("actF", [128, KT, MB], BF16)
        b_act = c.buf("actF")
        hT = c.sbuf("hT", [128, FT, MB + 1], BF16)
        b_hT = c.buf("hT")
        cwt, b_cw = load_const("cwt", [128, 3, 2 * FT], cw_d)
        cbt, b_cb = load_const("cbt", [128, 2 * FT], cb_d)
        flag, b_flag = load_const("flagF", [128, 2], flag_d)
        cwm = c.sbuf("cwm", [128, 3, 2 * FT], F32)
        c.op("vector", lambda e: e.tensor_scalar(cwm[:], cwt[:], flag[:, 1:2], None, ALU.mult), reads=[b_cw, b_flag], writes=[b_cw])
        carry = c.sbuf("carry", [128, 2 * FT, 2], F32)
        b_carry = c.buf("carry")
        c.op("vector", lambda e: e.memset(carry[:], 0.0), writes=[b_carry])
        EW = MB + 3
        Eb = [c.sbuf(f"Eb{i}", [128, EW], F32) for i in range(2)]
        b_E = c.bufs_n("Eb", 2)
        accA = [c.sbuf(f"accA{i}", [128, MB + 1], F32) for i in range(4)]
        b_accA = c.bufs_n("accA", 4)
        accV = [c.sbuf(f"accV{i}", [128, MB + 1], F32) for i in range(2)]
        b_accV = c.bufs_n("accV", 2)
        xl = [c.sbuf(f"xlF{i}", [128, 512], F32) for i in range(4)]
        b_xl = c.bufs_n("xlF", 4)
        cnt = {"e": 0, "x": 0}
        jb = (TOK // 2) // MB
        for blk in range(NB):
            t0 = blk * MB
            last_blk = (blk == NB - 1)
            nout = MB + 1 if last_blk else MB
            load_block(c, act, b_act, xn2T["ap"], t0, MB, KT)
            for f0 in range(0, FT, 4):
                nf = min(4, FT - f0)
                pend = {}

                def conv_epi(which):
                    def epi(ps, b_ps, ti, ts, tn, c0, ncol):
                        fcol = (c0 - which * DFF) // 128 + which * FT
                        i = cnt["e"] % 2
                        cnt["e"] += 1
                        E, bE = Eb[i], b_E[i]
                        if which == 0:
                            ia = (fcol - f0) % 4
                            A, bA = accA[ia], b_accA[ia]
                        else:
                            A, bA = accV[i], b_accV[i]
                        c.op("vector", lambda e: e.tensor_copy(E[:, 0:2], carry[:, fcol, :]), reads=[b_carry], writes=[bE])
                        c.op("scalar", lambda e: e.activation(E[:, 2:MB + 2], ps, AF.Copy), reads=[b_ps], writes=[bE])
                        if last_blk:
                            c.op("vector", lambda e: e.memset(E[:, MB + 2:MB + 3], 0.0), writes=[bE])
                        c.op("vector", lambda e: e.tensor_copy(carry[:, fcol, :], E[:, MB:MB + 2]), reads=[bE], writes=[b_carry])
                        c.op("vector", lambda e: e.tensor_scalar(A[:, 0:nout], E[:, 1:nout + 1], cwt[:, 1, fcol:fcol + 1], cbt[:, fcol:fcol + 1],
                                                                 ALU.mult, ALU.add), reads=[bE, b_cw, b_cb], writes=[bA])
                        c.op("vector", lambda e: e.scalar_tensor_tensor(A[:, 0:nout], E[:, 0:nout], cwt[:, 0, fcol:fcol + 1], A[:, 0:nout],
                                                                        ALU.mult, ALU.add), reads=[bE, b_cw], writes=[bA])
                        c.op("vector", lambda e: e.scalar_tensor_tensor(A[:, 0:nout], E[:, 2:nout + 2], cwt[:, 2, fcol:fcol + 1], A[:, 0:nout],
                                                                        ALU.mult, ALU.add), reads=[bE, b_cw], writes=[bA])
                        if blk == jb:
                            c.op("vector", lambda e: e.scalar_tensor_tensor(A[:, 0:1], E[:, 2:3], cwm[:, 2, fcol:fcol + 1], A[:, 0:1],
                                                                            ALU.mult, ALU.add), reads=[bE, b_cw], writes=[bA])
                            c.op("vector", lambda e: e.scalar_tensor_tensor(A[:, 1:2], E[:, 1:2], cwm[:, 0, fcol:fcol + 1], A[:, 1:2],
                                                                            ALU.mult, ALU.add), reads=[bE, b_cw], writes=[bA])
                        if which == 0:
                            c.op("scalar", lambda e: e.activation(A[:, 0:nout], A[:, 0:nout], AF.Gelu), reads=[], writes=[bA])
                            pend[fcol] = (A, bA)
                        else:
                            Aa, bAa = pend[fcol - FT]
                            c.op("vector", lambda e: e.tensor_tensor(hT[:, fcol - FT, 0:nout], Aa[:, 0:nout], A[:, 0:nout], ALU.mult),
                                 reads=[bAa, bA], writes=[b_hT])
                    return epi

                gemm(c, R, act, b_act, KT, MB, w_up, f0 * 128, nf * 128, "A", conv_epi(0))
                gemm(c, R, act, b_act, KT, MB, w_up, DFF + f0 * 128, nf * 128, "A", conv_epi(1))
            cstart = 1 if blk == 0 else 0
            tok_cols = []
            j = cstart
            while j < nout:
                n_ = min(128, nout - j)
                tok_cols.append((j, n_))
                j += n_

            def epiD(ps, b_ps, ti, ts, tn, c0, ncol):
                i = cnt["x"] % 4
                cnt["x"] += 1
                rows = slice(t0 - 1 + ts, t0 - 1 + ts + tn)
                c.dma("sync", xl[i][0:tn, 0:ncol], x1["ap"][rows, c0:c0 + ncol], b_xl[i], reads=[x1["buf"]], writes=[b_xl[i]])
                c.op("vector", lambda e: e.tensor_tensor(xl[i][0:tn, 0:ncol], ps, xl[i][0:tn, 0:ncol], ALU.add), reads=[b_ps], writes=[b_xl[i]])
                c.dma("sync", outb["ap"][rows, c0:c0 + ncol], xl[i][0:tn, 0:ncol], b_xl[i], reads=[b_xl[i]], writes=[outb["buf"]])

            gemm(c, R, hT, b_hT, FT, nout, w_down, 0, D, "B", epiD, tok_cols=tok_cols)
        c.end_phase()

    def phase_G():
        c.begin_phase()
        gbc, b_g = load_const("gfin", [128, D], g_fin)
        xt = [c.sbuf(f"gx{i}", [128, D], F32) for i in range(2)]
        b_xt = c.bufs_n("gx", 2)
        jk = [c.sbuf(f"gj{i}", [128, D], BF16) for i in range(2)]
        b_jk = c.bufs_n("gj", 2)
        ss = [c.sbuf(f"gss{i}", [128, 1], F32) for i in range(2)]
        b_ss = c.bufs_n("gss", 2)
        for tt in range(NT):
            s = tt % 2
            rows = slice(tt * 128, (tt + 1) * 128)
            c.dma("sync", xt[s][:], outb["ap"][rows, :], b_xt[s], reads=[outb["buf"]], writes=[b_xt[s]])
            c.op("scalar", lambda e: e.activation(jk[s][:], xt[s][:], AF.Square, accum_out=ss[s][:]), reads=[b_xt[s]], writes=[b_jk[s], b_ss[s]])
            rms_rstd(c, ss[s][:], b_ss[s], D)
            c.op("vector", lambda e: e.scalar_tensor_tensor(xt[s][:], xt[s][:], ss[s][:, 0:1], gbc[:], ALU.mult, ALU.mult),
                 reads=[b_ss[s], b_g], writes=[b_xt[s]])
            c.dma("gpsimd", outb["ap"][rows, :], xt[s][:], b_xt[s], reads=[b_xt[s]], writes=[outb["buf"]])
        c.end_phase()

    phase_norm_T({"ap": x, "buf": b_in}, g_mix, xnT)
    phase_A()
    phase_B()
    phase_C0()
    phase_D()
    phase_C()
    phase_E()
    phase_norm_T(x1, g_ffn, xn2T)
    phase_F()
    phase_G()
    c.barrier()
    stats = {k: v.ninst for k, v in c.q.items()}
    c.close()
    return nc, stats


def _dft_consts(TOK, two_seq):
    def cs(n):
        j = np.arange(n, dtype=np.int64)
        ang = 2.0 * np.pi * ((j[:, None] * j[None, :]) % n).astype(np.float64) / n
        return np.cos(ang) / np.sqrt(n), np.sin(ang) / np.sqrt(n)
    if two_seq:
        c1, s1 = cs(TOK // 2)
        cS = np.zeros((TOK, TOK)); sS = np.zeros((TOK, TOK))
        h = TOK // 2
        cS[:h, :h] = c1; cS[h:, h:] = c1; sS[:h, :h] = s1; sS[h:, h:] = s1
    else:
        cS, sS = cs(TOK)
    cC, sC = cs(1024)
    bf = ml_dtypes.bfloat16
    return cS.astype(np.float32).astype(bf), sS.astype(np.float32).astype(bf), cC.astype(np.float32).astype(bf), (-sC).astype(np.float32).astype(bf)


def make_core_inputs(xs, two_seq, W, TOK):
    f = np.float32
    r = np.arange(64)
    triF = -(r[:, None] <= r[None, :]).astype(f)
    triB = -(r[:, None] >= r[None, :]).astype(f)
    tri = np.stack([triF, triB], axis=1)
    flag = 0.0 if two_seq else 1.0
    dc, ds, cC, sCn = _dft_consts(TOK, two_seq)
    bc = lambda v: np.ascontiguousarray(np.broadcast_to(np.asarray(v, f).reshape(1, -1), (128, v.size)))
    bgate = np.concatenate([W["b_igate"].reshape(-1), W["b_fgate"].reshape(-1)]).astype(f)
    return {
        "x": np.ascontiguousarray(xs, dtype=f),
        "w_in": W["w_in"][0], "w_mo": W["w_mlstm_out"][0], "w_fo": W["w_fourier_out"][0], "w_out": W["w_out"][0],
        "w_up": W["w_up"][0], "w_down": W["w_down"][0],
        "g_mix": bc(W["norm_mix"][0]), "g_ffn": bc(W["norm_ffn"][0]), "g_fin": bc(W["norm_final"]), "g_mh": bc(W["mh_norm"][0]),
        "bgate": np.ascontiguousarray(np.broadcast_to(bgate.reshape(1, 32), (64, 32))),
        "bm": np.ascontiguousarray(W["b_merge"][0].reshape(64, 128).T.astype(f)),
        "cw": np.ascontiguousarray(W["conv_w"][0].reshape(3, 2 * FT, 128).transpose(2, 0, 1).astype(f)),
        "cb": np.ascontiguousarray(W["conv_b"][0].reshape(2 * FT, 128).T.astype(f)),
        "ident": np.eye(128, dtype=f),
        "tri": np.ascontiguousarray(tri), "mask": np.ascontiguousarray(-tri),
        "flag": np.ascontiguousarray(np.broadcast_to(np.array([[flag, flag - 1.0]], f), (128, 2))),
        "dftc": dc, "dfts": ds, "cc": cC, "scn": sCn,
    }


def kernel(**inputs):
    W = {k: np.asarray(v) for k, v in inputs.items() if k not in ("x_prompt", "x_sample")}
    xp = np.asarray(inputs["x_prompt"], dtype=np.float32)
    xs = np.asarray(inputs["x_sample"], dtype=np.float32)
    TOK = xp.shape[1]
    assert xs.shape[1] * 2 == TOK and xp.shape[0] == 4 and xs.shape[0] == 8
    nc, _ = build_program(TOK)
    in_maps = []
    for i in range(4):
        in_maps.append(make_core_inputs(xp[i], False, W, TOK))
    for i in range(4):
        in_maps.append(make_core_inputs(xs[2 * i:2 * i + 2].reshape(TOK, D), True, W, TOK))
    res = run_bass_kernel_spmd(nc, in_maps, core_ids=list(range(8)))
    outs = [np.asarray(r["out"], dtype=np.float32) for r in res.results]
    y_prompt = np.stack(outs[0:4], axis=0)
    y_sample = np.stack([o.reshape(2, TOK // 2, D) for o in outs[4:8]], axis=0).reshape(8, TOK // 2, D)
    return (y_prompt, y_sample)
```

```python
import math
from contextlib import ExitStack
import numpy as np
import ml_dtypes
import concourse.bass as bass
import concourse.mybir as mybir
from concourse.bass_utils import run_bass_kernel_spmd

F32 = mybir.dt.float32
BF16 = mybir.dt.bfloat16
AF = mybir.ActivationFunctionType
ALU = mybir.AluOpType

D = 4096
H = 8
DQK = 256
DV = 512
L = 64
OFF_Q, OFF_K, OFF_V, OFF_O, OFF_I, OFF_F, OFF_FR, OFF_GM, OFF_GF, IN_W = (
    0, 2048, 4096, 8192, 12288, 12304, 12320, 16416, 20512, 24608)
DFF = 11008
FT = DFF // 128
EPS = 1e-6
KT = D // 128
KTC = 8
CW = 512
NWS = 4


class Buf:
    __slots__ = ("name", "r", "w", "dslot")

    def __init__(self, name):
        self.name = name
        self.r = {}
        self.w = {}
        self.dslot = None


class SemSlot:
    __slots__ = ("sem", "count")

    def __init__(self, sem):
        self.sem = sem
        self.count = 0


def _merge(d, tok):
    sem, val = tok
    k = id(sem)
    if k not in d or d[k][1] < val:
        d[k] = (sem, val)


class EngQ:
    def __init__(self, ctx, eng, name):
        self.eng = eng
        self.name = name
        self.sem = ctx.es.enter_context(ctx.nc.semaphore("q_" + name))
        self.count = 0
        self.waited = {}
        self.ninst = 0

    def wait_tokens(self, toks):
        for sem, val in toks:
            if sem is self.sem and self.name == "tensor":
                continue
            k = id(sem)
            if self.waited.get(k, 0) < val:
                self.eng.wait_ge(sem, val)
                self.waited[k] = val
                self.ninst += 1


class Ctx:
    def __init__(self, nc):
        self.nc = nc
        self.es = ExitStack()
        self.q = {n: EngQ(self, getattr(nc, n), n) for n in ("tensor", "vector", "scalar", "gpsimd", "sync")}
        self.free_slots = []
        self.nslots = 0
        self.bufs = []
        self.pes = None
        self.uid = 0
        self.stq = "sync"

    def begin_phase(self):
        self.pes = ExitStack()
        self.phase_bufs = []

    def end_phase(self):
        self.barrier()
        for b in self.phase_bufs:
            if b.dslot is not None:
                self.free_slots.append(b.dslot)
                b.dslot = None
        self.pes.close()
        self.pes = None

    def sbuf(self, name, shape, dtype):
        self.uid += 1
        return self.pes.enter_context(self.nc.sbuf_tensor(f"{name}_{self.uid}", list(shape), dtype))

    def psum(self, name, shape, dtype=F32):
        self.uid += 1
        return self.pes.enter_context(self.nc.psum_tensor(f"{name}_{self.uid}", list(shape), dtype))

    def buf(self, name, persistent=False):
        b = Buf(name)
        self.bufs.append(b)
        if not persistent and self.pes is not None:
            self.phase_bufs.append(b)
        return b

    def bufs_n(self, name, n):
        return [self.buf(f"{name}{i}") for i in range(n)]

    def _slot(self, b):
        if b.dslot is None:
            if self.free_slots:
                b.dslot = self.free_slots.pop()
            else:
                self.nslots += 1
                b.dslot = SemSlot(self.es.enter_context(self.nc.semaphore(f"d{self.nslots}")))
        return b.dslot

    def _deps(self, reads, writes):
        d = {}
        for b in reads:
            for t in b.w.values():
                _merge(d, t)
        for b in writes:
            for t in b.w.values():
                _merge(d, t)
            for t in b.r.values():
                _merge(d, t)
        return list(d.values())

    def op(self, qname, fn, reads=(), writes=(), signal=True):
        q = self.q[qname]
        q.wait_tokens(self._deps(reads, writes))
        ins = fn(q.eng)
        q.ninst += 1
        if signal:
            q.count += 1
            ins.then_inc(q.sem, 1)
            tok = (q.sem, q.count)
        else:
            tok = (q.sem, q.count + 1)
        for b in reads:
            _merge(b.r, tok)
        for b in writes:
            _merge(b.w, tok)
        return tok

    def dma(self, qname, out, in_, owner, reads=(), writes=()):
        q = self.q[qname]
        slot = self._slot(owner)
        q.wait_tokens(self._deps(reads, writes))
        ins = q.eng.dma_start(out=out, in_=in_)
        q.ninst += 1
        slot.count += 16
        ins.then_inc(slot.sem, 16)
        tok = (slot.sem, slot.count)
        for b in reads:
            _merge(b.r, tok)
        for b in writes:
            _merge(b.w, tok)
        return tok

    def _all_tokens(self):
        d = {}
        for b in self.bufs:
            for t in b.w.values():
                _merge(d, t)
            for t in b.r.values():
                _merge(d, t)
        return list(d.values())

    def barrier(self):
        toks = self._all_tokens()
        for q in self.q.values():
            q.wait_tokens(toks)
        for b in self.bufs:
            b.r.clear()
            b.w.clear()
        self.bufs = [b for b in self.bufs if b not in self.phase_bufs] if self.pes is not None else self.bufs

    def close(self):
        self.es.close()


class Res:
    def __init__(self, c, wring=True, nstage=4):
        self.c = c
        self.ps = [c.psum(f"ps{i}", [128, 512], F32) for i in range(8)]
        self.b_ps = c.bufs_n("ps", 8)
        self.nacc = 0
        if wring:
            self.wt = [c.sbuf(f"wt{i}", [128, KTC, CW], BF16) for i in range(NWS)]
            self.b_wt = c.bufs_n("wt", NWS)
            self.nw = 0
        self.st32 = [c.sbuf(f"st32_{i}", [128, 512], F32) for i in range(nstage)]
        self.b_st32 = c.bufs_n("st32_", nstage)
        self.st16 = [c.sbuf(f"st16_{i}", [128, 512], BF16) for i in range(nstage)]
        self.b_st16 = c.bufs_n("st16_", nstage)
        self.nst = 0
        self.nev = 0

    def stage(self, dtype):
        i = self.nst % len(self.st32)
        self.nst += 1
        if dtype == F32:
            return self.st32[i], self.b_st32[i]
        return self.st16[i], self.b_st16[i]

    def evq(self):
        self.nev += 1
        return "scalar" if self.nev % 2 else "vector"


def copy_op(qn, out, in_):
    if qn == "scalar":
        return lambda e: e.activation(out, in_, AF.Copy)
    return lambda e: e.tensor_copy(out, in_)


def gemm(c, R, act, b_act, kt_n, mb, w_ap, col0, ncols, orient, epi, tok_cols=None):
    nkc = (kt_n + KTC - 1) // KTC
    if tok_cols is None:
        step = 512 if orient == "A" else 128
        tok_cols = [(s, min(step, mb - s)) for s in range(0, mb, step)]
    for c0 in range(col0, col0 + ncols, CW):
        cw = min(CW, col0 + ncols - c0)
        if orient == "A":
            accs = [(fi, ti) for fi in range((cw + 127) // 128) for ti in range(len(tok_cols))]
        else:
            accs = [(0, ti) for ti in range(len(tok_cols))]
        for a0 in range(0, len(accs), 4):
            grp = accs[a0:a0 + 4]
            base = 4 * (R.nacc % 2)
            R.nacc += 1
            for kc in range(nkc):
                k0 = kc * KTC
                kn = min(KTC, kt_n - k0)
                s = R.nw % NWS
                R.nw += 1
                src = w_ap[k0 * 128:(k0 + kn) * 128, c0:c0 + cw].rearrange("(kt p) n -> p kt n", p=128)
                c.dma("gpsimd", R.wt[s][:, 0:kn, 0:cw], src, R.b_wt[s], writes=[R.b_wt[s]])
                for gi, (fi, ti) in enumerate(grp):
                    ts, tn = tok_cols[ti]
                    bank = base + gi
                    for kk in range(kn):
                        kt = k0 + kk
                        last = (kt == kt_n - 1)
                        if orient == "A":
                            fn_ = min(128, cw - fi * 128)
                            c.op("tensor", lambda e: e.matmul(R.ps[bank][0:fn_, 0:tn], R.wt[s][:, kk, fi * 128:fi * 128 + fn_],
                                                              act[:, kt, ts:ts + tn], start=(kt == 0), stop=last),
                                 reads=[b_act, R.b_wt[s]], writes=[R.b_ps[bank]], signal=(last or kk == kn - 1))
                        else:
                            c.op("tensor", lambda e: e.matmul(R.ps[bank][0:tn, 0:cw], act[:, kt, ts:ts + tn],
                                                              R.wt[s][:, kk, 0:cw], start=(kt == 0), stop=last),
                                 reads=[b_act, R.b_wt[s]], writes=[R.b_ps[bank]], signal=(last or kk == kn - 1))
            for gi, (fi, ti) in enumerate(grp):
                ts, tn = tok_cols[ti]
                bank = base + gi
                if orient == "A":
                    fn_ = min(128, cw - fi * 128)
                    epi(R.ps[bank][0:fn_, 0:tn], R.b_ps[bank], ti, ts, tn, c0 + fi * 128, fn_)
                else:
                    epi(R.ps[bank][0:tn, 0:cw], R.b_ps[bank], ti, ts, tn, c0, cw)


def load_block(c, dst, b_dst, src, t0, mb, kt_n):
    for k0 in range(0, kt_n, 8):
        kn = min(8, kt_n - k0)
        c.dma("sync", dst[:, k0:k0 + kn, 0:mb],
              src[k0 * 128:(k0 + kn) * 128, t0:t0 + mb].rearrange("(kt p) t -> p kt t", p=128),
              b_dst, writes=[b_dst])


class TransposeStore:
    def __init__(self, c, R, ident, b_ident, dst):
        self.c, self.R, self.ident, self.b_ident, self.dst = c, R, ident, b_ident, dst
        self.stg = [c.sbuf(f"tstg{i}", [128, KT, 512], BF16) for i in range(2)]
        self.b_stg = c.bufs_n("tstg", 2)
        self.ng = 0

    def tile(self, xb, b_xb, tt, ntiles):
        c, R = self.c, self.R
        grp = tt // 4
        s = grp % 2
        j4 = tt % 4
        for g in range(KT // 8):
            bank = self.ng % 8
            self.ng += 1
            pst = R.ps[bank][:].bitcast(BF16)
            for j in range(8):
                kt = g * 8 + j
                c.op("tensor", lambda e: e.transpose(pst[:, j * 128:(j + 1) * 128], xb[:, kt * 128:(kt + 1) * 128], self.ident[:]),
                     reads=[b_xb, self.b_ident], writes=[R.b_ps[bank]], signal=(j == 7))
            qn = R.evq()
            c.op(qn, copy_op(qn, self.stg[s][:, g * 8:(g + 1) * 8, j4 * 128:(j4 + 1) * 128],
                             pst.rearrange("p (a b) -> p a b", a=8)),
                 reads=[R.b_ps[bank]], writes=[self.b_stg[s]])
        if j4 == 3 or tt == ntiles - 1:
            ncol = (j4 + 1) * 128
            t0 = grp * 512
            for k0 in range(0, KT, 8):
                c.dma("gpsimd", self.dst["ap"][k0 * 128:(k0 + 8) * 128, t0:t0 + ncol].rearrange("(kt p) t -> p kt t", p=128),
                      self.stg[s][:, k0:k0 + 8, 0:ncol], self.b_stg[s], reads=[self.b_stg[s]], writes=[self.dst["buf"]])


def rms_rstd(c, ss, b_ss, n):
    c.op("vector", lambda e: e.tensor_scalar(ss, ss, 1.0 / n, EPS, ALU.mult, ALU.add), reads=[b_ss], writes=[b_ss])
    c.op("scalar", lambda e: e.activation(ss, ss, AF.Sqrt), reads=[b_ss], writes=[b_ss])
    c.op("vector", lambda e: e.reciprocal(ss, ss), reads=[b_ss], writes=[b_ss])


def build_program(TOK, MB=512, stop_after=None):
    assert TOK % MB == 0 and MB == 512
    NB = TOK // MB
    NT = TOK // 128
    NCH = TOK // L
    nc = bass.Bass("TRN2", target_bir_lowering=False)

    def din(name, shape, dt=F32):
        return nc.dram_tensor(name, list(shape), dt, kind="ExternalInput").ap()

    x = din("x", [TOK, D])
    w_in = din("w_in", [D, IN_W])
    w_mo = din("w_mo", [D, D])
    w_fo = din("w_fo", [D, D])
    w_out = din("w_out", [D, D])
    w_up = din("w_up", [D, 2 * DFF])
    w_down = din("w_down", [DFF, D])
    g_mix = din("g_mix", [128, D])
    g_ffn = din("g_ffn", [128, D])
    g_fin = din("g_fin", [128, D])
    g_mh = din("g_mh", [128, D])
    bgate_d = din("bgate", [128, 32])
    bm_d = din("bm", [128, 64])
    cw_d = din("cw", [128, 3, 2 * FT])
    cb_d = din("cb", [128, 2 * FT])
    ident_d = din("ident", [128, 128])
    tri_d = din("tri", [64, 2, 64])
    mask_d = din("mask", [64, 2, 64])
    flag_d = din("flag", [128, 2])
    dftc_d = din("dftc", [TOK, TOK], BF16)
    dfts_d = din("dfts", [TOK, TOK], BF16)
    cc_d = din("cc", [1024, 1024], BF16)
    scn_d = din("scn", [1024, 1024], BF16)
    out = nc.dram_tensor("out", [TOK, D], F32, kind="ExternalOutput").ap()

    c = Ctx(nc)

    def scratch(name, shape, dt):
        return {"ap": nc.dram_tensor(name, list(shape), dt).ap(), "buf": c.buf(name, persistent=True)}

    xnT = scratch("s_xnT", [D, TOK], BF16)
    qT = scratch("s_qT", [2048, TOK], BF16)
    kT = scratch("s_kT", [2048, TOK], BF16)
    kk = scratch("s_kk", [TOK, 2048], BF16)
    vv = scratch("s_vv", [TOK, D], BF16)
    og = scratch("s_og", [TOK, D], F32)
    gif = scratch("s_gif", [TOK, 32], F32)
    frX = scratch("s_frX", [TOK, D], BF16)
    gmT = scratch("s_gmT", [D, TOK], F32)
    gfT = scratch("s_gfT", [D, TOK], F32)
    hdir = [scratch("s_hF", [TOK, D], F32), scratch("s_hB", [TOK, D], F32)]
    hgT = scratch("s_hgT", [D, TOK], BF16)
    zT = scratch("s_zT", [D, TOK], BF16)
    mgT = scratch("s_mgT", [D, TOK], BF16)
    x1 = scratch("s_x1", [TOK, D], F32)
    xn2T = scratch("s_xn2T", [D, TOK], BF16)
    outb = {"ap": out, "buf": c.buf("out", persistent=True)}
    b_in = c.buf("inputs", persistent=True)

    def load_const(name, shape, src, dt=F32, q="sync"):
        t = c.sbuf(name, shape, dt)
        b = c.buf(name)
        c.dma(q, t[:], src, b, writes=[b])
        return t, b

    def load_ident():
        idf, b_idf = load_const("identf", [128, 128], ident_d)
        idb = c.sbuf("identb", [128, 128], BF16)
        b_idb = c.buf("identb")
        c.op("vector", lambda e: e.tensor_copy(idb[:], idf[:]), reads=[b_idf], writes=[b_idb])
        return idb, b_idb

    def phase_norm_T(src, gsrc, dst):
        c.begin_phase()
        R = Res(c, wring=False, nstage=1)
        idb, b_idb = load_ident()
        gbc, b_g = load_const("gbc", [128, D], gsrc)
        TS = TransposeStore(c, R, idb, b_idb, dst)
        xt = [c.sbuf(f"xt{i}", [128, D], F32) for i in range(2)]
        b_xt = c.bufs_n("xt", 2)
        xb = [c.sbuf(f"xb{i}", [128, D], BF16) for i in range(2)]
        b_xb = c.bufs_n("xb", 2)
        ss = [c.sbuf(f"ss{i}", [128, 1], F32) for i in range(2)]
        b_ss = c.bufs_n("ss", 2)
        for tt in range(NT):
            s = tt % 2
            c.dma("sync", xt[s][:], src["ap"][tt * 128:(tt + 1) * 128, :], b_xt[s], reads=[src["buf"]], writes=[b_xt[s]])
            c.op("scalar", lambda e: e.activation(xb[s][:], xt[s][:], AF.Square, accum_out=ss[s][:]),
                 reads=[b_xt[s]], writes=[b_xb[s], b_ss[s]])
            rms_rstd(c, ss[s][:], b_ss[s], D)
            c.op("vector", lambda e: e.scalar_tensor_tensor(xb[s][:], xt[s][:], ss[s][:, 0:1], gbc[:], ALU.mult, ALU.mult),
                 reads=[b_xt[s], b_ss[s], b_g], writes=[b_xb[s]])
            TS.tile(xb[s], b_xb[s], tt, NT)
        c.end_phase()

    def phase_A():
        c.begin_phase()
        R = Res(c)
        act = c.sbuf("actA", [128, KT, MB], BF16)
        b_act = c.buf("actA")

        bg128, b_bg128 = load_const("bg128", [128, 32], bgate_d)

        def store_epi(dst, dt, orient):
            def epi(ps, b_ps, ti, ts, tn, c0, ncol, dst=dst, dt=dt):
                stg, b_stg = R.stage(dt)
                qn = R.evq()
                if dst is gif:
                    sv = stg[0:tn, 0:ncol]
                    dv = dst["ap"][t0 + ts:t0 + ts + tn, 0:ncol]
                    c.op("vector", lambda e: e.tensor_tensor(sv, ps, bg128[0:tn, 0:ncol], ALU.add), reads=[b_ps, b_bg128], writes=[b_stg])
                    c.dma("sync", dv, sv, b_stg, reads=[b_stg], writes=[dst["buf"]])
                    return
                if orient == "A":
                    sv = stg[0:ncol, 0:tn]
                    dv = dst["ap"][c0 - dst["c0"]:c0 - dst["c0"] + ncol, t0 + ts:t0 + ts + tn]
                else:
                    sv = stg[0:tn, 0:ncol]
                    dv = dst["ap"][t0 + ts:t0 + ts + tn, c0 - dst["c0"]:c0 - dst["c0"] + ncol]
                c.op(qn, copy_op(qn, sv, ps), reads=[b_ps], writes=[b_stg])
                c.dma("sync", dv, sv, b_stg, reads=[b_stg], writes=[dst["buf"]])
            return epi

        segs = [
            (OFF_Q, 2048, "A", qT, BF16), (OFF_K, 2048, "A", kT, BF16), (OFF_K, 2048, "B", kk, BF16),
            (OFF_V, 4096, "B", vv, BF16), (OFF_O, 4096, "B", og, F32), (OFF_I, 32, "B", gif, F32),
            (OFF_FR, 4096, "B", frX, BF16), (OFF_GM, 4096, "A", gmT, F32), (OFF_GF, 4096, "A", gfT, F32),
        ]
        for blk in range(NB):
            t0 = blk * MB
            load_block(c, act, b_act, xnT["ap"], t0, MB, KT)
            for (c0, n, orient, dst, dt) in segs:
                dst["c0"] = c0
                gemm(c, R, act, b_act, KT, MB, w_in, c0, n, orient, store_epi(dst, dt, orient))
        c.end_phase()

    def phase_B():
        c.begin_phase()
        ps = [c.psum(f"psB{i}", [128, 512], F32) for i in range(8)]
        b_psS = c.bufs_n("psS", H)
        b_psD = c.buf("psD")
        b_psn = c.buf("psn")
        b_psN = c.bufs_n("psN", 2)
        b_psC = c.bufs_n("psC", 4)
        tri, b_tri = load_const("tri", [64, 2, 64], tri_d)
        mask, b_mask = load_const("mask", [64, 2, 64], mask_d)
        flag, b_flag = load_const("flag", [128, 2], flag_d)
        gall, b_gall = load_const("gall", [64, NCH, 32], gif["ap"].rearrange("(c p) n -> p c n", p=64))
        onesn = c.sbuf("onesn", [64, 128], F32)
        ones16 = c.sbuf("ones16", [64, 1], BF16)
        b_ones = c.buf("ones")
        c.op("vector", lambda e: e.memset(onesn[:], -1.0), writes=[b_ones])
        c.op("vector", lambda e: e.memset(ones16[:], 1.0), writes=[b_ones])
        C32 = c.sbuf("C32", [128, H, 2, DV], F32)
        Cb = c.sbuf("Cb", [128, H, 2, DV], BF16)
        n32 = c.sbuf("n32", [128, 2 * H], F32)
        nb = c.sbuf("nb", [128, 2 * H], BF16)
        b_C = [c.buf(f"C{h}") for h in range(H)]
        b_Cb = [c.buf(f"Cb{h}") for h in range(H)]
        b_n = c.buf("n32")
        b_nb = c.buf("nb")
        qs = [c.sbuf(f"qs{i}", [128, 16, 128], BF16) for i in range(2)]
        ks = [c.sbuf(f"ks{i}", [128, 16, 128], BF16) for i in range(2)]
        kks = [c.sbuf(f"kks{i}", [64, 2, 2048], BF16) for i in range(2)]
        vvs = [c.sbuf(f"vvs{i}", [64, 2, D], BF16) for i in range(2)]
        b_in_s = c.bufs_n("mls_in", 2)
        hbuf = [c.sbuf(f"hbuf{i}", [64, D], F32) for i in range(2)]
        b_hbuf = c.bufs_n("hbuf", 2)
        vt = [c.sbuf(f"vt{i}", [64, DV], BF16) for i in range(H)]
        b_vt = c.bufs_n("vt", H)
        PT = [c.sbuf(f"PT{i}", [64, H, 64], BF16) for i in range(2)]
        b_PT = c.bufs_n("PT", 2)
        gt = [c.sbuf(f"gt{i}", [64, 32], F32) for i in range(2)]
        b_gt = c.bufs_n("gt", 2)
        spt = c.sbuf("spt", [64, 2, NCH * H], F32)
        Wt = c.sbuf("Wt", [64, NCH, H], F32)
        Et = c.sbuf("Et", [64, NCH, H], F32)
        At = c.sbuf("At", [64, NCH, H], F32)
        Eg = c.sbuf("Eg", [128, NCH, H], F32)
        Ab = c.sbuf("Ab", [64, NCH, H], BF16)
        b_gq = c.buf("gateq")
        for d in range(2):
            c.op("scalar", lambda e: e.activation(spt[:, d, :].rearrange("p (c h) -> p c h", h=H), gall[:, :, 16 + 8 * d:24 + 8 * d], AF.Exp, scale=-1.0),
                 reads=[b_gall], writes=[b_gq])
        c.op("scalar", lambda e: e.activation(spt[:], spt[:], AF.Ln, bias=1.0), reads=[], writes=[b_gq])
        for d in range(2):
            pb = ps[2][0:64, 0:NCH * H].rearrange("p (c h) -> p c h", h=H)
            pg = ps[3][:, 0:NCH * H].rearrange("p (c h) -> p c h", h=H)
            c.op("tensor", lambda e: e.matmul(ps[2][0:64, 0:NCH * H], tri[:, d, :], spt[:, d, :], start=True, stop=True),
                 reads=[b_tri, b_gq], writes=[b_psN[0]])
            c.op("tensor", lambda e: e.matmul(ps[3][:, 0:NCH * H], onesn[:], spt[:, d, :], start=True, stop=True),
                 reads=[b_ones, b_gq], writes=[b_psN[1]])
            c.op("vector", lambda e: e.tensor_tensor(Wt[:], gall[:, :, 8 * d:8 * d + 8], pb, ALU.subtract),
                 reads=[b_gall, b_psN[0]], writes=[b_gq])
            c.op("scalar", lambda e: e.activation(Wt[:], Wt[:], AF.Exp, bias=-math.log(16.0)), reads=[], writes=[b_gq])
            c.op("scalar", lambda e: e.activation(Et[:], pb, AF.Exp), reads=[b_psN[0]], writes=[b_gq])
            c.op("scalar", lambda e: e.activation(Eg[:], pg, AF.Exp), reads=[b_psN[1]], writes=[b_gq])
            c.op("vector", lambda e: e.tensor_tensor(At[:], Wt[:], Eg[0:64, :, :], ALU.mult), reads=[], writes=[b_gq])
            c.op("vector", lambda e: e.tensor_copy(Ab[:], At[:]), reads=[], writes=[b_gq])
            c.op("vector", lambda e: e.memset(C32[:], 0.0), writes=b_C)
            c.op("vector", lambda e: e.memset(Cb[:], 0.0), writes=b_Cb)
            c.op("vector", lambda e: e.memset(n32[:], 0.0), writes=[b_n])
            c.op("vector", lambda e: e.memset(nb[:], 0.0), writes=[b_nb])
            order = list(range(NCH)) if d == 0 else list(range(NCH - 1, -1, -1))
            reset_at = NCH // 2 if d == 0 else NCH // 2 - 1

            def load_sc(step_):
                if step_ >= NCH:
                    return
                sc = order[step_] // 2
                s = (step_ // 2) % 2
                t0 = sc * 128
                bi = b_in_s[s]
                c.dma("sync", qs[s][:], qT["ap"][:, t0:t0 + 128].rearrange("(kt p) t -> p kt t", p=128), bi, reads=[qT["buf"]], writes=[bi])
                c.dma("sync", ks[s][:], kT["ap"][:, t0:t0 + 128].rearrange("(kt p) t -> p kt t", p=128), bi, reads=[kT["buf"]], writes=[bi])
                c.dma("sync", kks[s][:], kk["ap"][t0:t0 + 128, :].rearrange("(c p) n -> p c n", p=64), bi, reads=[kk["buf"]], writes=[bi])
                c.dma("sync", vvs[s][:], vv["ap"][t0:t0 + 128, :].rearrange("(c p) n -> p c n", p=64), bi, reads=[vv["buf"]], writes=[bi])

            load_sc(0)
            for step, ci in enumerate(order):
                lc = ci % 2
                s = (step // 2) % 2
                if step % 2 == 0:
                    load_sc(step + 2)
                bi = b_in_s[s]
                cs = slice(lc * 64, lc * 64 + 64)
                if ci == reset_at:
                    c.op("vector", lambda e: e.tensor_scalar(C32[:], C32[:], flag[:, 0:1], None, ALU.mult), reads=[b_flag], writes=b_C)
                    c.op("vector", lambda e: e.tensor_scalar(Cb[:], Cb[:], flag[:, 0:1], None, ALU.mult), reads=[b_flag], writes=b_Cb)
                    c.op("vector", lambda e: e.tensor_scalar(n32[:], n32[:], flag[:, 0:1], None, ALU.mult), reads=[b_flag], writes=[b_n])
                    c.op("vector", lambda e: e.tensor_scalar(nb[:], nb[:], flag[:, 0:1], None, ALU.mult), reads=[b_flag], writes=[b_nb])
                wt_, et, eg, at, a_b = Wt[:, ci, :], Et[:, ci, :], Eg[:, ci, :], At[:, ci, :], Ab[:, ci, :]
                g_ = gt[step % 2]
                bg = b_gt[step % 2]
                dn, dn2, rr = g_[:, 0:8], g_[:, 8:16], g_[:, 16:24]
                for h in range(H):
                    c.op("scalar", lambda e: e.activation(vt[h][:], vvs[s][:, lc, h * DV:(h + 1) * DV], AF.Copy, scale=at[:, h:h + 1]),
                         reads=[bi, b_gq], writes=[b_vt[h]])
                pt = PT[step % 2]
                bpt = b_PT[step % 2]
                for h in range(H):
                    for dt in range(2):
                        c.op("tensor", lambda e: e.matmul(ps[0][0:64, h * 64:(h + 1) * 64], ks[s][:, 2 * h + dt, cs], qs[s][:, 2 * h + dt, cs],
                                                          start=(dt == 0), stop=(dt == 1)),
                             reads=[bi], writes=[b_psS[h]], signal=(dt == 1))
                for h in range(H):
                    c.op("vector", lambda e: e.scalar_tensor_tensor(pt[:, h, :], ps[0][0:64, h * 64:(h + 1) * 64], wt_[:, h:h + 1], mask[:, d, :],
                                                                    ALU.mult, ALU.mult),
                         reads=[b_psS[h], b_gq, b_mask], writes=[bpt])
                for h in range(H):
                    c.op("tensor", lambda e: e.matmul(ps[1][0:64, 16 + h:17 + h], pt[:, h, :], ones16[:], start=True, stop=False),
                         reads=[bpt, b_ones], writes=[b_psD], signal=False)
                    for dt in range(2):
                        c.op("tensor", lambda e: e.matmul(ps[1][0:64, 16 + h:17 + h], qs[s][:, 2 * h + dt, cs], nb[:, 2 * h + dt:2 * h + dt + 1],
                                                          start=False, stop=(dt == 1)),
                             reads=[bi, b_nb], writes=[b_psD], signal=(dt == 1))
                c.op("vector", lambda e: e.tensor_tensor(dn, ps[1][0:64, 16:24], et, ALU.mult), reads=[b_psD, b_gq], writes=[bg])
                c.op("vector", lambda e: e.tensor_scalar(dn2, dn, -1.0, None, ALU.mult), reads=[], writes=[bg])
                c.op("vector", lambda e: e.tensor_tensor(dn, dn, dn2, ALU.max), reads=[], writes=[bg])
                c.op("vector", lambda e: e.tensor_scalar(dn, dn, 1.0, None, ALU.max), reads=[], writes=[bg])
                c.op("vector", lambda e: e.reciprocal(dn, dn), reads=[], writes=[bg])
                c.op("vector", lambda e: e.tensor_tensor(rr, dn, et, ALU.mult), reads=[], writes=[bg])
                hb = hbuf[step % 2]
                bhb = b_hbuf[step % 2]
                for h in range(H):
                    pn = 2 + (h % 2)
                    c.op("tensor", lambda e: e.matmul(ps[pn][0:64, :], pt[:, h, :], vvs[s][:, lc, h * DV:(h + 1) * DV], start=True, stop=False),
                         reads=[bpt, bi], writes=[b_psN[h % 2]], signal=False)
                    for dt in range(2):
                        c.op("tensor", lambda e: e.matmul(ps[pn][0:64, :], qs[s][:, 2 * h + dt, cs], Cb[:, h, dt, :], start=False, stop=(dt == 1)),
                             reads=[bi, b_Cb[h]], writes=[b_psN[h % 2]], signal=(dt == 1))
                    c.op("scalar", lambda e: e.activation(hb[:, h * DV:(h + 1) * DV], ps[pn][0:64, :], AF.Copy, scale=rr[:, h:h + 1]),
                         reads=[b_psN[h % 2], bg], writes=[bhb])
                    for dt in range(2):
                        pc = 4 + 2 * (h % 2) + dt
                        kslice = kks[s][:, lc, h * DQK + dt * 128:h * DQK + dt * 128 + 128]
                        c.op("tensor", lambda e: e.matmul(ps[pc][:, :], kslice, vt[h][:], start=True, stop=True),
                             reads=[bi, b_vt[h]], writes=[b_psC[pc - 4]])
                        c.op("tensor", lambda e: e.matmul(ps[1][:, 24 + 2 * h + dt:25 + 2 * h + dt], kslice, a_b[:, h:h + 1], start=True, stop=True),
                             reads=[bi, b_gq], writes=[b_psn])
                        c.op("vector", lambda e: e.scalar_tensor_tensor(C32[:, h, dt, :], C32[:, h, dt, :], eg[:, h:h + 1], ps[pc][:, :],
                                                                        ALU.mult, ALU.add),
                             reads=[b_gq, b_psC[pc - 4]], writes=[b_C[h]])
                        c.op("scalar", copy_op("scalar", Cb[:, h, dt, :], C32[:, h, dt, :]), reads=[b_C[h]], writes=[b_Cb[h]])
                n3 = n32[:].rearrange("p (h t) -> p h t", t=2)
                p3 = ps[1][:, 24:40].rearrange("p (h t) -> p h t", t=2)
                for dt in range(2):
                    c.op("vector", lambda e: e.tensor_tensor(n3[:, :, dt], n3[:, :, dt], eg, ALU.mult), reads=[b_gq], writes=[b_n])
                    c.op("vector", lambda e: e.tensor_tensor(n3[:, :, dt], n3[:, :, dt], p3[:, :, dt], ALU.add), reads=[b_psn], writes=[b_n])
                c.op("vector", lambda e: e.tensor_copy(nb[:], n32[:]), reads=[b_n], writes=[b_nb])
                c.dma("gpsimd", hdir[d]["ap"][ci * 64:(ci + 1) * 64, :], hb[:], bhb, reads=[bhb], writes=[hdir[d]["buf"]])
        c.end_phase()

    def phase_C0():
        c.begin_phase()
        R = Res(c, wring=False, nstage=1)
        idb, b_idb = load_ident()
        gbc, b_g = load_const("gmh", [128, D], g_mh)
        TS = TransposeStore(c, R, idb, b_idb, hgT)
        ha = [c.sbuf(f"ha{i}", [128, D], F32) for i in range(2)]
        hb_ = [c.sbuf(f"hb{i}", [128, D], F32) for i in range(2)]
        ho = [c.sbuf(f"ho{i}", [128, D], F32) for i in range(2)]
        b_ha = c.bufs_n("ha", 2)
        b_hb = c.bufs_n("hb", 2)
        b_ho = c.bufs_n("ho", 2)
        xb = [c.sbuf(f"hx{i}", [128, D], BF16) for i in range(2)]
        b_xb = c.bufs_n("hx", 2)
        ss = [c.sbuf(f"hss{i}", [128, H], F32) for i in range(2)]
        b_ss = c.bufs_n("hss", 2)
        for tt in range(NT):
            s = tt % 2
            rows = slice(tt * 128, (tt + 1) * 128)
            c.dma("sync", ha[s][:], hdir[0]["ap"][rows, :], b_ha[s], reads=[hdir[0]["buf"]], writes=[b_ha[s]])
            c.dma("sync", hb_[s][:], hdir[1]["ap"][rows, :], b_hb[s], reads=[hdir[1]["buf"]], writes=[b_hb[s]])
            c.dma("sync", ho[s][:], og["ap"][rows, :], b_ho[s], reads=[og["buf"]], writes=[b_ho[s]])
            c.op("vector", lambda e: e.tensor_tensor(ha[s][:], ha[s][:], hb_[s][:], ALU.add), reads=[b_hb[s]], writes=[b_ha[s]])
            for h in range(H):
                c.op("scalar", lambda e: e.activation(hb_[s][:, h * DV:(h + 1) * DV], ha[s][:, h * DV:(h + 1) * DV], AF.Square,
                                                      accum_out=ss[s][:, h:h + 1]),
                     reads=[b_ha[s]], writes=[b_hb[s], b_ss[s]])
            rms_rstd(c, ss[s][:], b_ss[s], DV)
            c.op("scalar", lambda e: e.activation(ho[s][:], ho[s][:], AF.Sigmoid), reads=[], writes=[b_ho[s]])
            for h in range(H):
                c.op("vector", lambda e: e.tensor_scalar(ha[s][:, h * DV:(h + 1) * DV], ha[s][:, h * DV:(h + 1) * DV], ss[s][:, h:h + 1], None, ALU.mult),
                     reads=[b_ss[s]], writes=[b_ha[s]])
            c.op("vector", lambda e: e.tensor_tensor(ho[s][:], ho[s][:], gbc[:], ALU.mult), reads=[b_g], writes=[b_ho[s]])
            c.op("vector", lambda e: e.tensor_tensor(xb[s][:], ha[s][:], ho[s][:], ALU.mult), reads=[b_ha[s], b_ho[s]], writes=[b_xb[s]])
            TS.tile(xb[s], b_xb[s], tt, NT)
        c.end_phase()

    def phase_D():
        c.begin_phase()
        R = Res(c, wring=False)
        ST = TOK // 128
        TB = 256
        Xg = c.sbuf("Xg", [128, ST, 1024], BF16)
        b_Xg = c.buf("Xg")
        dcs = [c.sbuf(f"dc{i}", [128, ST, TB], BF16) for i in range(2)]
        dss = [c.sbuf(f"ds{i}", [128, ST, TB], BF16) for i in range(2)]
        b_dft = c.bufs_n("dft", 2)
        ccs, b_cc = load_const("ccs", [128, 8, 1024], cc_d.rearrange("(kt p) n -> p kt n", p=128), BF16)
        scs, b_sc = load_const("scs", [128, 8, 1024], scn_d.rearrange("(kt p) n -> p kt n", p=128), BF16)
        A1 = [c.sbuf(f"A1_{i}", [128, 8, TB], BF16) for i in range(2)]
        A2 = [c.sbuf(f"A2_{i}", [128, 8, TB], BF16) for i in range(2)]
        b_A = c.bufs_n("Aseq", 2)
        nit = 0
        nbk = 0
        NTB = TOK // TB

        def load_dft(i):
            if i >= 4 * NTB:
                return
            s = i % 2
            tb = i % NTB
            for s0 in range(0, ST, 8):
                c.dma("sync", dcs[s][:, s0:s0 + 8, :], dftc_d[s0 * 128:(s0 + 8) * 128, tb * TB:(tb + 1) * TB].rearrange("(st p) n -> p st n", p=128),
                      b_dft[s], writes=[b_dft[s]])
                c.dma("sync", dss[s][:, s0:s0 + 8, :], dfts_d[s0 * 128:(s0 + 8) * 128, tb * TB:(tb + 1) * TB].rearrange("(st p) n -> p st n", p=128),
                      b_dft[s], writes=[b_dft[s]])

        for g in range(4):
            for s0 in range(0, ST, 8):
                c.dma("sync", Xg[:, s0:s0 + 8, :], frX["ap"][s0 * 128:(s0 + 8) * 128, g * 1024:(g + 1) * 1024].rearrange("(st p) n -> p st n", p=128),
                      b_Xg, reads=[frX["buf"]], writes=[b_Xg])
            for tb in range(TOK // TB):
                s = nit % 2
                if nit == 0:
                    load_dft(0)
                load_dft(nit + 1)
                nit += 1
                for ct in range(8):
                    b1 = nbk % 8
                    b2 = (nbk + 1) % 8
                    nbk += 2
                    for st in range(ST):
                        c.op("tensor", lambda e: e.matmul(R.ps[b1][:, 0:TB], Xg[:, st, ct * 128:(ct + 1) * 128], dcs[s][:, st, :],
                                                          start=(st == 0), stop=(st == ST - 1)),
                             reads=[b_Xg, b_dft[s]], writes=[R.b_ps[b1]], signal=(st == ST - 1))
                        c.op("tensor", lambda e: e.matmul(R.ps[b2][:, 0:TB], Xg[:, st, ct * 128:(ct + 1) * 128], dss[s][:, st, :],
                                                          start=(st == 0), stop=(st == ST - 1)),
                             reads=[b_Xg, b_dft[s]], writes=[R.b_ps[b2]], signal=(st == ST - 1))
                    c.op("scalar", copy_op("scalar", A1[s][:, ct, :], R.ps[b1][:, 0:TB]), reads=[R.b_ps[b1]], writes=[b_A[s]])
                    c.op("vector", copy_op("vector", A2[s][:, ct, :], R.ps[b2][:, 0:TB]), reads=[R.b_ps[b2]], writes=[b_A[s]])
                for c2 in range(8):
                    bk = nbk % 8
                    nbk += 1
                    for ct in range(8):
                        c.op("tensor", lambda e: e.matmul(R.ps[bk][:, 0:TB], ccs[:, ct, c2 * 128:(c2 + 1) * 128], A1[s][:, ct, :],
                                                          start=(ct == 0), stop=False),
                             reads=[b_cc, b_A[s]], writes=[R.b_ps[bk]], signal=False)
                        c.op("tensor", lambda e: e.matmul(R.ps[bk][:, 0:TB], scs[:, ct, c2 * 128:(c2 + 1) * 128], A2[s][:, ct, :],
                                                          start=False, stop=(ct == 7)),
                             reads=[b_sc, b_A[s]], writes=[R.b_ps[bk]], signal=(ct == 7))
                    stg, b_stg = R.stage(BF16)
                    qn = R.evq()
                    c.op(qn, copy_op(qn, stg[:, 0:TB], R.ps[bk][:, 0:TB]), reads=[R.b_ps[bk]], writes=[b_stg])
                    r0 = g * 1024 + c2 * 128
                    c.dma("gpsimd", zT["ap"][r0:r0 + 128, tb * TB:(tb + 1) * TB], stg[:, 0:TB], b_stg, reads=[b_stg], writes=[zT["buf"]])
        c.end_phase()

    def phase_C():
        c.begin_phase()
        R = Res(c)
        actm = c.sbuf("actm", [128, KT, MB], BF16)
        actf = c.sbuf("actf", [128, KT, MB], BF16)
        b_am = c.buf("actm")
        b_af = c.buf("actf")
        bm, b_bm = load_const("bm", [128, 64], bm_d)
        t1 = [c.sbuf(f"t1_{i}", [128, 512], F32) for i in range(8)]
        b_t1 = c.bufs_n("t1_", 8)
        gl = [c.sbuf(f"gl{i}", [128, 512], F32) for i in range(4)]
        b_gl = c.bufs_n("gl", 4)
        cnt = {"g": 0, "t": 0}
        for blk in range(NB):
            t0 = blk * MB
            load_block(c, actm, b_am, hgT["ap"], t0, MB, KT)
            load_block(c, actf, b_af, zT["ap"], t0, MB, KT)

            def gate(src, boff, c0, ncol, ts, tn):
                i = cnt["g"] % 4
                cnt["g"] += 1
                c.dma("sync", gl[i][0:ncol, 0:tn], src["ap"][c0:c0 + ncol, t0 + ts:t0 + ts + tn], b_gl[i], reads=[src["buf"]], writes=[b_gl[i]])
                ft = boff + c0 // 128
                c.op("scalar", lambda e: e.activation(gl[i][0:ncol, 0:tn], gl[i][0:ncol, 0:tn], AF.Sigmoid, bias=bm[0:ncol, ft:ft + 1]),
                     reads=[b_bm], writes=[b_gl[i]])
                return gl[i], b_gl[i]

            for c0 in range(0, D, CW):
                slots = {}

                def epi1(ps, b_ps, ti, ts, tn, cc0, ncol):
                    g_, bg_ = gate(gmT, 0, cc0, ncol, ts, tn)
                    i = cnt["t"] % 8
                    cnt["t"] += 1
                    slots[(cc0, ti)] = i
                    c.op("vector", lambda e: e.tensor_tensor(t1[i][0:ncol, 0:tn], ps, g_[0:ncol, 0:tn], ALU.mult),
                         reads=[b_ps, bg_], writes=[b_t1[i]])

                def epi2(ps, b_ps, ti, ts, tn, cc0, ncol):
                    g_, bg_ = gate(gfT, 32, cc0, ncol, ts, tn)
                    i = slots[(cc0, ti)]
                    c.op("vector", lambda e: e.tensor_tensor(g_[0:ncol, 0:tn], ps, g_[0:ncol, 0:tn], ALU.mult),
                         reads=[b_ps], writes=[bg_])
                    stg, b_stg = R.stage(BF16)
                    c.op("vector", lambda e: e.tensor_tensor(stg[0:ncol, 0:tn], g_[0:ncol, 0:tn], t1[i][0:ncol, 0:tn], ALU.add),
                         reads=[bg_, b_t1[i]], writes=[b_stg])
                    c.dma("sync", mgT["ap"][cc0:cc0 + ncol, t0 + ts:t0 + ts + tn], stg[0:ncol, 0:tn], b_stg, reads=[b_stg], writes=[mgT["buf"]])

                gemm(c, R, actm, b_am, KT, MB, w_mo, c0, CW, "A", epi1)
                gemm(c, R, actf, b_af, KT, MB, w_fo, c0, CW, "A", epi2)
        c.end_phase()

    def phase_E():
        c.begin_phase()
        R = Res(c)
        act = c.sbuf("actE", [128, KT, MB], BF16)
        b_act = c.buf("actE")
        xl = [c.sbuf(f"xl{i}", [128, 512], F32) for i in range(4)]
        b_xl = c.bufs_n("xl", 4)
        cnt = {"x": 0}
        for blk in range(NB):
            t0 = blk * MB
            load_block(c, act, b_act, mgT["ap"], t0, MB, KT)

            def epi(ps, b_ps, ti, ts, tn, c0, ncol):
                i = cnt["x"] % 4
                cnt["x"] += 1
                rows = slice(t0 + ts, t0 + ts + tn)
                c.dma("sync", xl[i][0:tn, 0:ncol], x[rows, c0:c0 + ncol], b_xl[i], writes=[b_xl[i]])
                c.op("vector", lambda e: e.tensor_tensor(xl[i][0:tn, 0:ncol], ps, xl[i][0:tn, 0:ncol], ALU.add), reads=[b_ps], writes=[b_xl[i]])
                c.dma("sync", x1["ap"][rows, c0:c0 + ncol], xl[i][0:tn, 0:ncol], b_xl[i], reads=[b_xl[i]], writes=[x1["buf"]])

            gemm(c, R, act, b_act, KT, MB, w_out, 0, D, "B", epi)
        c.end_phase()

    def phase_F():
        c.begin_phase()
        R = Res(c, nstage=1)
        act = c.sbuf("actF", [128, KT, MB], BF16)
        b_act = c.buf("actF")
        hT = c.sbuf("hT", [128, FT, MB + 1], BF16)
        b_hT = c.buf("hT")
        cwt, b_cw = load_const("cwt", [128, 3, 2 * FT], cw_d)
        cbt, b_cb = load_const("cbt", [128, 2 * FT], cb_d)
        flag, b_flag = load_const("flagF", [128, 2], flag_d)
        cwm = c.sbuf("cwm", [128, 3, 2 * FT], F32)
        c.op("vector", lambda e: e.tensor_scalar(cwm[:], cwt[:], flag[:, 1:2], None, ALU.mult), reads=[b_cw, b_flag], writes=[b_cw])
        carry = c.sbuf("carry", [128, 2 * FT, 2], F32)
        b_carry = c.buf("carry")
        c.op("vector", lambda e: e.memset(carry[:], 0.0), writes=[b_carry])
        EW = MB + 3
        Eb = [c.sbuf(f"Eb{i}", [128, EW], F32) for i in range(2)]
        b_E = c.bufs_n("Eb", 2)
        accA = [c.sbuf(f"accA{i}", [128, MB + 1], F32) for i in range(4)]
        b_accA = c.bufs_n("accA", 4)
        accV = [c.sbuf(f"accV{i}", [128, MB + 1], F32) for i in range(2)]
        b_accV = c.bufs_n("accV", 2)
        xl = [c.sbuf(f"xlF{i}", [128, 512], F32) for i in range(4)]
        b_xl = c.bufs_n("xlF", 4)
        cnt = {"e": 0, "x": 0}
        jb = (TOK // 2) // MB
        for blk in range(NB):
            t0 = blk * MB
            last_blk = (blk == NB - 1)
            nout = MB + 1 if last_blk else MB
            load_block(c, act, b_act, xn2T["ap"], t0, MB, KT)
            for f0 in range(0, FT, 4):
                nf = min(4, FT - f0)
                pend = {}

                def conv_epi(which):
                    def epi(ps, b_ps, ti, ts, tn, c0, ncol):
                        fcol = (c0 - which * DFF) // 128 + which * FT
                        i = cnt["e"] % 2
                        cnt["e"] += 1
                        E, bE = Eb[i], b_E[i]
                        if which == 0:
                            ia = (fcol - f0) % 4
                            A, bA = accA[ia], b_accA[ia]
                        else:
                            A, bA = accV[i], b_accV[i]
                        c.op("vector", lambda e: e.tensor_copy(E[:, 0:2], carry[:, fcol, :]), reads=[b_carry], writes=[bE])
                        c.op("scalar", lambda e: e.activation(E[:, 2:MB + 2], ps, AF.Copy), reads=[b_ps], writes=[bE])
                        if last_blk:
                            c.op("vector", lambda e: e.memset(E[:, MB + 2:MB + 3], 0.0), writes=[bE])
                        c.op("vector", lambda e: e.tensor_copy(carry[:, fcol, :], E[:, MB:MB + 2]), reads=[bE], writes=[b_carry])
                        c.op("vector", lambda e: e.tensor_scalar(A[:, 0:nout], E[:, 1:nout + 1], cwt[:, 1, fcol:fcol + 1], cbt[:, fcol:fcol + 1],
                                                                 ALU.mult, ALU.add), reads=[bE, b_cw, b_cb], writes=[bA])
                        c.op("vector", lambda e: e.scalar_tensor_tensor(A[:, 0:nout], E[:, 0:nout], cwt[:, 0, fcol:fcol + 1], A[:, 0:nout],
                                                                        ALU.mult, ALU.add), reads=[bE, b_cw], writes=[bA])
                        c.op("vector", lambda e: e.scalar_tensor_tensor(A[:, 0:nout], E[:, 2:nout + 2], cwt[:, 2, fcol:fcol + 1], A[:, 0:nout],
                                                                        ALU.mult, ALU.add), reads=[bE, b_cw], writes=[bA])
                        if blk == jb:
                            c.op("vector", lambda e: e.scalar_tensor_tensor(A[:, 0:1], E[:, 2:3], cwm[:, 2, fcol:fcol + 1], A[:, 0:1],
                                                                            ALU.mult, ALU.add), reads=[bE, b_cw], writes=[bA])
                            c.op("vector", lambda e: e.scalar_tensor_tensor(A[:, 1:2], E[:, 1:2], cwm[:, 0, fcol:fcol + 1], A[:, 1:2],
                                                                            ALU.mult, ALU.add), reads=[bE, b_cw], writes=[bA])
                        if which == 0:
                            c.op("scalar", lambda e: e.activation(A[:, 0:nout], A[:, 0:nout], AF.Gelu), reads=[], writes=[bA])
                            pend[fcol] = (A, bA)
                        else:
                            Aa, bAa = pend[fcol - FT]
                            c.op("vector", lambda e: e.tensor_tensor(hT[:, fcol - FT, 0:nout], Aa[:, 0:nout], A[:, 0:nout], ALU.mult),
                                 reads=[bAa, bA], writes=[b_hT])
                    return epi

                gemm(c, R, act, b_act, KT, MB, w_up, f0 * 128, nf * 128, "A", conv_epi(0))
                gemm(c, R, act, b_act, KT, MB, w_up, DFF + f0 * 128, nf * 128, "A", conv_epi(1))
            cstart = 1 if blk == 0 else 0
            tok_cols = []
            j = cstart
            while j < nout:
                n_ = min(128, nout - j)
                tok_cols.append((j, n_))
                j += n_

            def epiD(ps, b_ps, ti, ts, tn, c0, ncol):
                i = cnt["x"] % 4
                cnt["x"] += 1
                rows = slice(t0 - 1 + ts, t0 - 1 + ts + tn)
                c.dma("sync", xl[i][0:tn, 0:ncol], x1["ap"][rows, c0:c0 + ncol], b_xl[i], reads=[x1["buf"]], writes=[b_xl[i]])
                c.op("vector", lambda e: e.tensor_tensor(xl[i][0:tn, 0:ncol], ps, xl[i][0:tn, 0:ncol], ALU.add), reads=[b_ps], writes=[b_xl[i]])
                c.dma("sync", outb["ap"][rows, c0:c0 + ncol], xl[i][0:tn, 0:ncol], b_xl[i], reads=[b_xl[i]], writes=[outb["buf"]])

            gemm(c, R, hT, b_hT, FT, nout, w_down, 0, D, "B", epiD, tok_cols=tok_cols)
        c.end_phase()

    def phase_G():
        c.begin_phase()
        gbc, b_g = load_const("gfin", [128, D], g_fin)
        xt = [c.sbuf(f"gx{i}", [128, D], F32) for i in range(2)]
        b_xt = c.bufs_n("gx", 2)
        jk = [c.sbuf(f"gj{i}", [128, D], BF16) for i in range(2)]
        b_jk = c.bufs_n("gj", 2)
        ss = [c.sbuf(f"gss{i}", [128, 1], F32) for i in range(2)]
        b_ss = c.bufs_n("gss", 2)
        for tt in range(NT):
            s = tt % 2
            rows = slice(tt * 128, (tt + 1) * 128)
            c.dma("sync", xt[s][:], outb["ap"][rows, :], b_xt[s], reads=[outb["buf"]], writes=[b_xt[s]])
            c.op("scalar", lambda e: e.activation(jk[s][:], xt[s][:], AF.Square, accum_out=ss[s][:]), reads=[b_xt[s]], writes=[b_jk[s], b_ss[s]])
            rms_rstd(c, ss[s][:], b_ss[s], D)
            c.op("vector", lambda e: e.scalar_tensor_tensor(xt[s][:], xt[s][:], ss[s][:, 0:1], gbc[:], ALU.mult, ALU.mult),
                 reads=[b_ss[s], b_g], writes=[b_xt[s]])
            c.dma("gpsimd", outb["ap"][rows, :], xt[s][:], b_xt[s], reads=[b_xt[s]], writes=[outb["buf"]])
        c.end_phase()

    plist = [("A0", lambda: phase_norm_T({"ap": x, "buf": b_in}, g_mix, xnT)), ("A", phase_A), ("B", phase_B), ("C0", phase_C0),
             ("D", phase_D), ("C", phase_C), ("E", phase_E), ("F0", lambda: phase_norm_T(x1, g_ffn, xn2T)), ("F", phase_F), ("G", phase_G)]
    for pname, pf in plist:
        pf()
        if stop_after == pname:
            break
    c.barrier()
    stats = {k: v.ninst for k, v in c.q.items()}
    c.close()
    return nc, stats


def _dft_consts(TOK, two_seq):
    def cs(n):
        j = np.arange(n, dtype=np.int64)
        ang = 2.0 * np.pi * ((j[:, None] * j[None, :]) % n).astype(np.float64) / n
        return np.cos(ang) / np.sqrt(n), np.sin(ang) / np.sqrt(n)
    if two_seq:
        c1, s1 = cs(TOK // 2)
        cS = np.zeros((TOK, TOK)); sS = np.zeros((TOK, TOK))
        h = TOK // 2
        cS[:h, :h] = c1; cS[h:, h:] = c1; sS[:h, :h] = s1; sS[h:, h:] = s1
    else:
        cS, sS = cs(TOK)
    cC, sC = cs(1024)
    bf = ml_dtypes.bfloat16
    return cS.astype(np.float32).astype(bf), sS.astype(np.float32).astype(bf), cC.astype(np.float32).astype(bf), (-sC).astype(np.float32).astype(bf)


def make_core_inputs(xs, two_seq, W, TOK):
    f = np.float32
    r = np.arange(64)
    triF = -(r[:, None] <= r[None, :]).astype(f)
    triB = -(r[:, None] >= r[None, :]).astype(f)
    tri = np.stack([triF, triB], axis=1)
    flag = 0.0 if two_seq else 1.0
    dc, ds, cC, sCn = _dft_consts(TOK, two_seq)
    bc = lambda v: np.ascontiguousarray(np.broadcast_to(np.asarray(v, f).reshape(1, -1), (128, v.size)))
    bgate = np.concatenate([W["b_igate"].reshape(-1), W["b_fgate"].reshape(-1)]).astype(f)
    return {
        "x": np.ascontiguousarray(xs, dtype=f),
        "w_in": W["w_in"][0], "w_mo": W["w_mlstm_out"][0], "w_fo": W["w_fourier_out"][0], "w_out": W["w_out"][0],
        "w_up": W["w_up"][0], "w_down": W["w_down"][0],
        "g_mix": bc(W["norm_mix"][0]), "g_ffn": bc(W["norm_ffn"][0]), "g_fin": bc(W["norm_final"]), "g_mh": bc(W["mh_norm"][0]),
        "bgate": np.ascontiguousarray(np.broadcast_to(bgate.reshape(1, 32), (128, 32))),
        "bm": np.ascontiguousarray(W["b_merge"][0].reshape(64, 128).T.astype(f)),
        "cw": np.ascontiguousarray(W["conv_w"][0].reshape(3, 2 * FT, 128).transpose(2, 0, 1).astype(f)),
        "cb": np.ascontiguousarray(W["conv_b"][0].reshape(2 * FT, 128).T.astype(f)),
        "ident": np.eye(128, dtype=f),
        "tri": np.ascontiguousarray(tri), "mask": np.ascontiguousarray(-tri),
        "flag": np.ascontiguousarray(np.broadcast_to(np.array([[flag, flag - 1.0]], f), (128, 2))),
        "dftc": dc, "dfts": ds, "cc": cC, "scn": sCn,
    }


def kernel(**inputs):
    W = {k: np.asarray(v) for k, v in inputs.items() if k not in ("x_prompt", "x_sample")}
    xp = np.asarray(inputs["x_prompt"], dtype=np.float32)
    xs = np.asarray(inputs["x_sample"], dtype=np.float32)
    TOK = xp.shape[1]
    assert xs.shape[1] * 2 == TOK and xp.shape[0] == 4 and xs.shape[0] == 8
    nc, _ = build_program(TOK)
    in_maps = []
    for i in range(4):
        in_maps.append(make_core_inputs(xp[i], False, W, TOK))
    for i in range(4):
        in_maps.append(make_core_inputs(xs[2 * i:2 * i + 2].reshape(TOK, D), True, W, TOK))
    res = run_bass_kernel_spmd(nc, in_maps, core_ids=list(range(8)))
    outs = [np.asarray(r["out"], dtype=np.float32) for r in res.results]
    y_prompt = np.stack(outs[0:4], axis=0)
    y_sample = np.stack([o.reshape(2, TOK // 2, D) for o in outs[4:8]], axis=0).reshape(8, TOK // 2, D)
    return (y_prompt, y_sample)
```
